# Optimizing a Trainium2 kernel written in Bass

```python
import math
import jax, jax.numpy as jnp
from jax import lax
import numpy as np

D_MODEL = 1024
BATCH = 4
SEQ = 4096
DEPTH = 1
DEC_BATCH = 128
DEC_SEQ = 8
PAST_LEN = 2048
PAGE_SIZE = 128

HEAD_DIM = 64
D_MIX = D_MODEL
H_NSA = (D_MIX // 2) // HEAD_DIM
H_FOX = (D_MIX // 2) // HEAD_DIM
G_NSA = 2
R_NSA = H_NSA // G_NSA
CMP_LEN = 32
CMP_STRIDE = 16
CMP_HID = 2 * HEAD_DIM
SEL_BLOCK = 64
SEL_TOP = 16
WINDOW = 512
Q_BLOCK = 128
ROPE_THETA = 10000.0
EPS = 1e-6
NEG = -1e30
FORCED = 1e4

SIZES = [H_NSA * HEAD_DIM] + [G_NSA * HEAD_DIM] * 6 + [3 * H_NSA, H_NSA * HEAD_DIM] + [H_FOX * HEAD_DIM] * 3 + [H_FOX, H_FOX * HEAD_DIM]
SPLIT_POINTS = tuple(int(v) for v in np.cumsum(SIZES)[:-1])
N_IN = int(sum(SIZES))

kernel_name = 'nsa_fox_hymba_decode_step'


def rms_norm(x, g):
    x32 = x.astype(jnp.float32)
    y = x32 * lax.rsqrt(jnp.mean(x32 * x32, axis=-1, keepdims=True) + EPS)
    return (y * g.astype(jnp.float32)).astype(x.dtype)


def rope(x, pos):
    half = HEAD_DIM // 2
    inv = jnp.power(jnp.float32(ROPE_THETA), -jnp.arange(half, dtype=jnp.float32) / half)
    ang = pos.astype(jnp.float32)[:, None] * inv[None, :]
    cos = jnp.cos(ang)[None, :, None, :]
    sin = jnp.sin(ang)[None, :, None, :]
    x32 = x.astype(jnp.float32)
    x1, x2 = x32[..., :half], x32[..., half:]
    return jnp.concatenate([x1 * cos - x2 * sin, x2 * cos + x1 * sin], axis=-1).astype(x.dtype)


def masked_softmax(s, mask):
    p = jax.nn.softmax(jnp.where(mask, s, NEG), axis=-1)
    return jnp.where(mask, p, 0.0)


def compress(x, pe, w1, w2):
    L = x.shape[1]
    n_cmp = (L - CMP_LEN) // CMP_STRIDE + 1
    idx = np.arange(n_cmp)[:, None] * CMP_STRIDE + np.arange(CMP_LEN)[None, :]
    blocks = x[:, idx] + pe[None, None, :, None, :]
    h = jax.nn.silu(jnp.einsum('bnlgd,ldh->bngh', blocks, w1))
    return jnp.einsum('bngh,hd->bngd', h, w2)


def cmp_to_sel(n_cmp, n_sel):
    cs = np.arange(n_cmp) * CMP_STRIDE
    ss = np.arange(n_sel) * SEL_BLOCK
    ov = np.clip(np.minimum(cs[:, None] + CMP_LEN, ss[None, :] + SEL_BLOCK) - np.maximum(cs[:, None], ss[None, :]), 0, None)
    return jnp.asarray(ov / CMP_LEN, dtype=jnp.float32)


def to_sel_blocks(t):
    B, L = t.shape[:2]
    n_sel = -(-L // SEL_BLOCK)
    t = jnp.pad(t, ((0, 0), (0, n_sel * SEL_BLOCK - L), (0, 0), (0, 0)))
    return t.reshape(B, n_sel, SEL_BLOCK, G_NSA, HEAD_DIM).transpose(0, 3, 1, 2, 4)


def nsa_core(q, gates, q_pos, kc, vc, ks_blk, vs_blk, kw, vw, kw_pos):
    B, Tq = q.shape[:2]
    scale = HEAD_DIM ** -0.5
    n_cmp = kc.shape[1]
    c_end = jnp.arange(n_cmp, dtype=jnp.int32) * CMP_STRIDE + CMP_LEN - 1
    c_mask = c_end[None, :] <= q_pos[:, None]
    s = jnp.einsum('bqgrd,bngd->bgrqn', q, kc).astype(jnp.float32) * scale
    p_c = masked_softmax(s, c_mask)
    o_c = jnp.einsum('bgrqn,bngd->bqgrd', p_c.astype(vc.dtype), vc)
    n_sel = ks_blk.shape[2]
    imp = jnp.einsum('bgrqn,nj->bgqj', p_c, cmp_to_sel(n_cmp, n_sel))
    blk = jnp.arange(n_sel, dtype=jnp.int32)
    forced = (blk[None, :] == 0) | (blk[None, :] == (q_pos // SEL_BLOCK)[:, None])
    causal = blk[None, :] * SEL_BLOCK <= q_pos[:, None]
    imp = jnp.where(forced, FORCED, jnp.where(causal, imp, -1.0))
    n_top = min(SEL_TOP, n_sel)
    _, idx = lax.top_k(imp, n_top)
    gather = jax.vmap(jax.vmap(lambda kb, ib: kb[ib]))
    ksel = gather(ks_blk, idx).reshape(B, G_NSA, Tq, n_top * SEL_BLOCK, HEAD_DIM)
    vsel = gather(vs_blk, idx).reshape(B, G_NSA, Tq, n_top * SEL_BLOCK, HEAD_DIM)
    kpos = (idx[..., None] * SEL_BLOCK + jnp.arange(SEL_BLOCK, dtype=jnp.int32)).reshape(B, G_NSA, Tq, n_top * SEL_BLOCK)
    s_mask = (kpos <= q_pos[None, None, :, None])[:, :, None]
    s = jnp.einsum('bqgrd,bgqkd->bgrqk', q, ksel).astype(jnp.float32) * scale
    p_s = masked_softmax(s, s_mask)
    o_s = jnp.einsum('bgrqk,bgqkd->bqgrd', p_s.astype(vsel.dtype), vsel)
    dist = q_pos[:, None] - kw_pos[None, :]
    w_mask = (dist >= 0) & (dist <= WINDOW) & (kw_pos[None, :] >= 0)
    s = jnp.einsum('bqgrd,bkgd->bgrqk', q, kw).astype(jnp.float32) * scale
    p_w = masked_softmax(s, w_mask)
    o_w = jnp.einsum('bgrqk,bkgd->bqgrd', p_w.astype(vw.dtype), vw)
    return gates[..., 0:1] * o_c + gates[..., 1:2] * o_s + gates[..., 2:3] * o_w


def fox_core(q, cq, q_pos, k, v, ck, k_pos):
    s = jnp.einsum('bqhd,bkhd->bhqk', q, k).astype(jnp.float32) * (HEAD_DIM ** -0.5)
    s = s + (jnp.transpose(cq, (0, 2, 1))[:, :, :, None] - jnp.transpose(ck, (0, 2, 1))[:, :, None, :])
    p = masked_softmax(s, k_pos[None, :] <= q_pos[:, None])
    return jnp.einsum('bhqk,bkhd->bqhd', p.astype(v.dtype), v)


def mixer_inputs(x, pos, g_norm, w_in, b_f, gq_a, gk_slc, gk_win, gq_b, gk_b):
    B, T, _ = x.shape
    h = rms_norm(x, g_norm)
    proj = jnp.einsum('btd,dn->btn', h, w_in)
    qa, kc, vc, ks, vs, kw, vw, ga, za, qb, kb, vb, fb, zb = jnp.split(proj, SPLIT_POINTS, axis=-1)
    kvh = lambda t: t.reshape(B, T, G_NSA, HEAD_DIM)
    fh = lambda t: t.reshape(B, T, H_FOX, HEAD_DIM)
    qa = rope(rms_norm(qa.reshape(B, T, H_NSA, HEAD_DIM), gq_a), pos).reshape(B, T, G_NSA, R_NSA, HEAD_DIM)
    kc = rope(kvh(kc), pos)
    ks = rope(rms_norm(kvh(ks), gk_slc), pos)
    kw = rope(rms_norm(kvh(kw), gk_win), pos)
    ga = jax.nn.sigmoid(ga).reshape(B, T, G_NSA, R_NSA, 3)
    qb = rms_norm(fh(qb), gq_b)
    kb = rms_norm(fh(kb), gk_b)
    logf = jax.nn.log_sigmoid(fb.astype(jnp.float32) + b_f.astype(jnp.float32))
    return qa, kc, kvh(vc), ks, kvh(vs), kw, kvh(vw), ga, za, qb, kb, fh(vb), logf, zb


def merge_out(x, oa, za, ob, zb, w_out):
    B, T, _ = x.shape
    u = jnp.concatenate([oa.reshape(B, T, -1) * jax.nn.silu(za), ob.reshape(B, T, -1) * jax.nn.silu(zb)], axis=-1)
    return x + jnp.einsum('btm,md->btd', u, w_out).astype(x.dtype)


def prompt_layer(x, lw):
    g_norm, w_in, b_f, gq_a, gk_cmp, gk_slc, gk_win, pe_k, pe_v, w1k, w2k, w1v, w2v, gq_b, gk_b, w_out = lw
    B, T, _ = x.shape
    pos = jnp.arange(T, dtype=jnp.int32)
    qa, kc, vc, ks, vs, kw, vw, ga, za, qb, kb, vb, logf, zb = mixer_inputs(x, pos, g_norm, w_in, b_f, gq_a, gk_slc, gk_win, gq_b, gk_b)
    kcmp = rms_norm(compress(kc, pe_k, w1k, w2k), gk_cmp)
    vcmp = compress(vc, pe_v, w1v, w2v)
    ks_blk, vs_blk = to_sel_blocks(ks), to_sel_blocks(vs)
    pad = ((0, 0), (WINDOW, 0), (0, 0), (0, 0))
    kw_pad, vw_pad = jnp.pad(kw, pad), jnp.pad(vw, pad)
    c = jnp.cumsum(logf, axis=1)
    n_blk = T // Q_BLOCK

    def nsa_block(i):
        s0 = i * Q_BLOCK
        q_pos = s0 + jnp.arange(Q_BLOCK, dtype=jnp.int32)
        kw_pos = s0 - WINDOW + jnp.arange(WINDOW + Q_BLOCK, dtype=jnp.int32)
        return nsa_core(lax.dynamic_slice_in_dim(qa, s0, Q_BLOCK, 1), lax.dynamic_slice_in_dim(ga, s0, Q_BLOCK, 1), q_pos,
                        kcmp, vcmp, ks_blk, vs_blk,
                        lax.dynamic_slice_in_dim(kw_pad, s0, WINDOW + Q_BLOCK, 1),
                        lax.dynamic_slice_in_dim(vw_pad, s0, WINDOW + Q_BLOCK, 1), kw_pos)

    def fox_block(i):
        s0 = i * Q_BLOCK
        q_pos = s0 + jnp.arange(Q_BLOCK, dtype=jnp.int32)
        return fox_core(lax.dynamic_slice_in_dim(qb, s0, Q_BLOCK, 1), lax.dynamic_slice_in_dim(c, s0, Q_BLOCK, 1), q_pos,
                        kb, vb, c, pos)

    blocks = jnp.arange(n_blk, dtype=jnp.int32)
    oa = jnp.moveaxis(lax.map(nsa_block, blocks), 0, 1).reshape(B, T, -1)
    ob = jnp.moveaxis(lax.map(fox_block, blocks), 0, 1).reshape(B, T, -1)
    y = merge_out(x, oa, za, ob, zb, w_out)
    wb = min(WINDOW, T)
    return (y, jnp.stack([kc, vc], axis=2), jnp.stack([ks, vs], axis=2), jnp.stack([kw[:, T - wb:], vw[:, T - wb:]], axis=2),
            jnp.stack([kb, vb], axis=2), logf)


def sample_layer(x, cmp_pool, slc_pool, win_buf, fox_pool, logf_pool, page_table, lw):
    g_norm, w_in, b_f, gq_a, gk_cmp, gk_slc, gk_win, pe_k, pe_v, w1k, w2k, w1v, w2v, gq_b, gk_b, w_out = lw
    DB, S, _ = x.shape
    past = page_table.shape[1] * PAGE_SIZE
    wb = win_buf.shape[1]
    pos = past + jnp.arange(S, dtype=jnp.int32)
    qa, kc, vc, ks, vs, kw, vw, ga, za, qb, kb, vb, logf, zb = mixer_inputs(x, pos, g_norm, w_in, b_f, gq_a, gk_slc, gk_win, gq_b, gk_b)

    def gather_pages(pool):
        return pool[page_table].reshape((DB, past) + pool.shape[2:])

    cmp_past = gather_pages(cmp_pool)
    slc_past = gather_pages(slc_pool)
    fox_past = gather_pages(fox_pool)
    logf_past = gather_pages(logf_pool)
    kc_all = jnp.concatenate([cmp_past[:, :, 0], kc], axis=1)
    vc_all = jnp.concatenate([cmp_past[:, :, 1], vc], axis=1)
    ks_all = jnp.concatenate([slc_past[:, :, 0], ks], axis=1)
    vs_all = jnp.concatenate([slc_past[:, :, 1], vs], axis=1)
    kw_all = jnp.concatenate([win_buf[:, :, 0], kw], axis=1)
    vw_all = jnp.concatenate([win_buf[:, :, 1], vw], axis=1)
    kcmp = rms_norm(compress(kc_all, pe_k, w1k, w2k), gk_cmp)
    vcmp = compress(vc_all, pe_v, w1v, w2v)
    kw_pos = past - wb + jnp.arange(wb + S, dtype=jnp.int32)
    oa = nsa_core(qa, ga, pos, kcmp, vcmp, to_sel_blocks(ks_all), to_sel_blocks(vs_all), kw_all, vw_all, kw_pos)
    c = jnp.cumsum(jnp.concatenate([logf_past.astype(jnp.float32), logf], axis=1), axis=1)
    kb_all = jnp.concatenate([fox_past[:, :, 0], kb], axis=1)
    vb_all = jnp.concatenate([fox_past[:, :, 1], vb], axis=1)
    ob = fox_core(qb, c[:, past:], pos, kb_all, vb_all, c, jnp.arange(past + S, dtype=jnp.int32))
    y = merge_out(x, oa, za, ob, zb, w_out)
    new_win = jnp.stack([kw_all[:, -wb:], vw_all[:, -wb:]], axis=2)
    return (y, jnp.stack([kc, vc], axis=2), jnp.stack([ks, vs], axis=2), new_win, jnp.stack([kb, vb], axis=2), logf)


def setup_inputs(seed: int = 0) -> dict:
    key = jax.random.key(seed)
    k = jax.random.split(key, 24)
    n_pages = PAST_LEN // PAGE_SIZE
    n_used = DEC_BATCH * n_pages
    n_pool = n_used + max(1, n_used // 4)
    wb = min(WINDOW, PAST_LEN)
    nrm = jax.random.normal
    f32 = jnp.float32
    gain = lambda kk, shape: 1.0 + 0.02 * nrm(kk, shape, f32)
    return {
        'x_prompt': nrm(k[0], (BATCH, SEQ, D_MODEL), f32),
        'x_sample': nrm(k[1], (DEC_BATCH, DEC_SEQ, D_MODEL), f32),
        'cache_nsa_cmp_kv': nrm(k[2], (DEPTH, n_pool, PAGE_SIZE, 2, G_NSA, HEAD_DIM), f32),
        'cache_nsa_slc_kv': nrm(k[3], (DEPTH, n_pool, PAGE_SIZE, 2, G_NSA, HEAD_DIM), f32),
        'cache_nsa_win_kv': nrm(k[4], (DEPTH, DEC_BATCH, wb, 2, G_NSA, HEAD_DIM), f32),
        'cache_fox_kv': nrm(k[5], (DEPTH, n_pool, PAGE_SIZE, 2, H_FOX, HEAD_DIM), f32),
        'cache_fox_logf': jax.nn.log_sigmoid(2.0 + 0.5 * nrm(k[6], (DEPTH, n_pool, PAGE_SIZE, H_FOX), f32)),
        'page_table': jax.random.permutation(k[7], n_pool)[:n_used].reshape(DEC_BATCH, n_pages).astype(jnp.int32),
        'g_norm': gain(k[8], (DEPTH, D_MODEL)),
        'w_in': nrm(k[9], (DEPTH, D_MODEL, N_IN), f32) * D_MODEL ** -0.5,
        'b_f': 2.0 + 0.5 * nrm(k[10], (DEPTH, H_FOX), f32),
        'gq_a': gain(k[11], (DEPTH, HEAD_DIM)),
        'gk_cmp': gain(k[12], (DEPTH, HEAD_DIM)),
        'gk_slc': gain(k[13], (DEPTH, HEAD_DIM)),
        'gk_win': gain(k[14], (DEPTH, HEAD_DIM)),
        'pe_cmp_k': 0.1 * nrm(k[15], (DEPTH, CMP_LEN, HEAD_DIM), f32),
        'pe_cmp_v': 0.1 * nrm(k[16], (DEPTH, CMP_LEN, HEAD_DIM), f32),
        'w_cmp1_k': nrm(k[17], (DEPTH, CMP_LEN, HEAD_DIM, CMP_HID), f32) * (CMP_LEN * HEAD_DIM) ** -0.5,
        'w_cmp2_k': nrm(k[18], (DEPTH, CMP_HID, HEAD_DIM), f32) * CMP_HID ** -0.5,
        'w_cmp1_v': nrm(k[19], (DEPTH, CMP_LEN, HEAD_DIM, CMP_HID), f32) * (CMP_LEN * HEAD_DIM) ** -0.5,
        'w_cmp2_v': nrm(k[20], (DEPTH, CMP_HID, HEAD_DIM), f32) * CMP_HID ** -0.5,
        'gq_b': gain(k[21], (DEPTH, HEAD_DIM)),
        'gk_b': gain(k[22], (DEPTH, HEAD_DIM)),
        'w_out': nrm(k[23], (DEPTH, D_MIX, D_MODEL), f32) * D_MIX ** -0.5,
    }


def reference(x_prompt, x_sample, cache_nsa_cmp_kv, cache_nsa_slc_kv, cache_nsa_win_kv, cache_fox_kv, cache_fox_logf,
              page_table, g_norm, w_in, b_f, gq_a, gk_cmp, gk_slc, gk_win, pe_cmp_k, pe_cmp_v,
              w_cmp1_k, w_cmp2_k, w_cmp1_v, w_cmp2_v, gq_b, gk_b, w_out):
    yp, ys = x_prompt, x_sample
    pst = [[] for _ in range(5)]
    sst = [[] for _ in range(5)]
    for l in range(DEPTH):
        lw = (g_norm[l], w_in[l], b_f[l], gq_a[l], gk_cmp[l], gk_slc[l], gk_win[l], pe_cmp_k[l], pe_cmp_v[l],
              w_cmp1_k[l], w_cmp2_k[l], w_cmp1_v[l], w_cmp2_v[l], gq_b[l], gk_b[l], w_out[l])
        yp, *p_new = prompt_layer(yp, lw)
        ys, *s_new = sample_layer(ys, cache_nsa_cmp_kv[l], cache_nsa_slc_kv[l], cache_nsa_win_kv[l],
                                  cache_fox_kv[l], cache_fox_logf[l], page_table, lw)
        for j in range(5):
            pst[j].append(p_new[j])
            sst[j].append(s_new[j])
    p_cmp, p_slc, p_win, p_fox, p_logf = [jnp.stack(a, axis=0) for a in pst]
    s_cmp, s_slc, s_win, s_fox, s_logf = [jnp.stack(a, axis=0) for a in sst]
    return (yp, ys, p_cmp, s_cmp, p_slc, s_slc, p_win, s_win, p_fox, s_fox, p_logf, s_logf)
```

```python
from contextlib import ExitStack
import os
import numpy as np
import ml_dtypes
import concourse.bass as bass
import concourse.mybir as mybir
from concourse.bass_utils import run_bass_kernel_spmd

F32 = mybir.dt.float32
BF16 = mybir.dt.bfloat16
I32 = mybir.dt.int32
ALU = mybir.AluOpType
AF = mybir.ActivationFunctionType
AX = mybir.AxisListType

NCORES = 8
D = 1024
T = 4096
NT = 16
NTILE = NT + 1
EPS = 1e-6
O_KC, O_KS, O_KW, O_KB, O_VC, O_VS, O_VW, O_VB, O_FB = 0, 128, 256, 384, 896, 1024, 1152, 1280, 1792
O_QA, O_QB, O_ZA, O_ZB, O_GA = 1800, 2312, 2824, 3336, 3848
NIN = 3872
CHUNKS = [(0, 512), (512, 512), (1024, 512), (1536, 264), (1800, 512), (2312, 512), (2824, 512), (3336, 512), (3848, 24)]
ENGS = ["sp", "act", "dve", "pool", "pe"]
SEM_LIMIT = 30000


class Sched:
    def __init__(self, nc, stack):
        self.nc = nc
        self.stack = stack
        self.q = {e: [] for e in ENGS}
        self.waited = {e: {} for e in ENGS}
        self.ctr = {}
        self.lastw = {}
        self.readers = {}
        self.nsem = 0
        self.all_tokens = {}
        self.dma_sems = set()

    def _newsem(self, name):
        self.nsem += 1
        return self.stack.enter_context(self.nc.semaphore(f"s{self.nsem}_{name}"))

    def _token(self, key, inc):
        c = self.ctr.get(key)
        if c is None or c[1] + inc > SEM_LIMIT:
            c = [self._newsem(key), 0]
            self.ctr[key] = c
        c[1] += inc
        tok = (c[0], c[1])
        self.all_tokens[id(c[0])] = tok
        return tok

    def op(self, eng, fn, r=(), w=(), dma=None):
        deps = []
        for b in r:
            if b in self.lastw:
                deps.append(self.lastw[b])
        for b in w:
            if b in self.lastw:
                deps.append(self.lastw[b])
            deps.extend(self.readers.get(b, ()))
        waits = {}
        for (s, v) in deps:
            k = id(s)
            if k in self.dma_sems:
                v = self.all_tokens[k][1]
            if self.waited[eng].get(k, 0) >= v:
                continue
            if k not in waits or waits[k][1] < v:
                waits[k] = (s, v)
        for k, (s, v) in waits.items():
            self.waited[eng][k] = v
        if dma is not None:
            tok = self._token("d_" + dma, 16)
            self.dma_sems.add(id(tok[0]))
        else:
            tok = self._token("e_" + eng, 1)
        self.q[eng].append((list(waits.values()), fn, tok, 16 if dma is not None else 1))
        for b in w:
            self.lastw[b] = tok
            self.readers[b] = []
        for b in r:
            if b not in w:
                self.readers.setdefault(b, []).append(tok)
        return tok

    def emit(self, block):
        nc = self.nc
        final = list(self.all_tokens.values())

        def run(engname, e):
            for waits, fn, tok, inc in self.q[engname]:
                for (s, v) in waits:
                    e.wait_ge(s, v)
                fn(e).then_inc(tok[0], inc)
            if engname == "sp":
                for (s, v) in final:
                    e.wait_ge(s, v)

        @block.sync
        def _(e):
            run("sp", e)

        @block.scalar
        def _(e):
            run("act", e)

        @block.vector
        def _(e):
            run("dve", e)

        @block.gpsimd
        def _(e):
            run("pool", e)

        @block.tensor
        def _(e):
            run("pe", e)


NS = 33
NQ = 17
QW = 2072
NPOOL = 2560
N_SAMPLE_SEQ = 16


def build(npq=16, nseq=N_SAMPLE_SEQ):
    nc = bass.Bass("TRN2", target_bir_lowering=False)
    dt = lambda n, s, d=F32, k="ExternalInput": nc.dram_tensor(n, list(s), d, kind=k).ap()
    x_in = dt("x", [NS * 128, D])
    w_in = dt("w_in", [D, NIN])
    w_out_in = dt("w_out", [D, D])
    gn_in = dt("gn", [128, 8])
    cs_in = dt("cs", [128, NS, 64])
    gk_in = dt("gk", [128, 12 * 64])
    gq_in = dt("gq", [128, 16 * 64])
    gkc_in = dt("gkc", [128, 64])
    bf_in = dt("bfb", [128, 8])
    win_in = dt("win", [16, 512, 256])
    w1_in = [dt("w1k", [128, 4096]), dt("w1v", [128, 4096])]
    w2_in = dt("w2", [128, 128])
    pe_in = dt("peT", [64, 64])
    msel_in = dt("msel", [128, 128])
    cval_in = dt("cval", [128, 2])
    f0_in = dt("f0", [128, 128])
    dmin_in = dt("dmin", [128, 128])
    if nseq > 0:
        pool_cmp = dt("pool_cmp", [NPOOL * 128, 256])
        pool_slc = dt("pool_slc", [NPOOL * 128, 256])
        pool_fox = dt("pool_fox", [NPOOL * 128, 1024])
        pool_lf = dt("pool_lf", [NPOOL * 128, 8])
    pt_in = dt("pt", [128, 256], I32)
    pio_in = dt("piota", [128, 1])
    tval_in = dt("tval", [128, 1])
    o_cmp = dt("o_cmp", [NQ * 128, 256], F32, "ExternalOutput")
    o_slc = dt("o_slc", [NQ * 128, 256], F32, "ExternalOutput")
    o_win = dt("o_win", [NQ * 128, 256], F32, "ExternalOutput")
    o_fox = dt("o_fox", [NQ * 128, 1024], F32, "ExternalOutput")
    o_lf = dt("o_lf", [NQ * 128, 8], F32, "ExternalOutput")
    o_wins = dt("o_wins", [16, 512, 256], F32, "ExternalOutput")
    o_y = dt("o_y", [NQ * 128, D], F32, "ExternalOutput")
    sc_cmp = dt("sc_cmp", [NS * 128, 256], F32, "Internal")
    sc_slc = dt("sc_slc", [NS * 128, 256], F32, "Internal")
    sc_win = dt("sc_win", [NS * 128, 256], F32, "Internal")
    sc_fox = dt("sc_fox", [NS * 128, 1024], F32, "Internal")
    sc_lf = dt("sc_lf", [128, 8], F32, "Internal")
    qstore = dt("qstore", [NQ * 128, QW], F32, "Internal")

    with ExitStack() as st:
        sb = lambda n, s, d=F32: st.enter_context(nc.sbuf_tensor("sb_" + n, list(s), d))
        ps = lambda n, s, d=F32: st.enter_context(nc.psum_tensor("ps_" + n, list(s), d))
        S = Sched(nc, st)

        def V(eng, meth, r, w, **kw):
            def fn(e):
                try:
                    return getattr(e, meth)(**kw)
                except Exception:
                    print("FAILED OP", eng, meth, {k: (getattr(v, "shape", v), getattr(v, "ap", None)) for k, v in kw.items()})
                    raise
            return S.op(eng, fn, r=r, w=w)

        def dma(eng, out, in_, r, w, tag):
            return S.op(eng, lambda e: e.dma_start(out=out, in_=in_), r=r, w=w, dma=tag)

        ident = sb("ident", [128, 128], BF16)
        triU = sb("triU", [128, 128])
        ones_f = sb("ones_f", [128, 128])
        sel127 = sb("sel127", [128, 128])
        arena = sb("arena", [128, 8 * NIN], BF16)
        w_bf = arena[:].rearrange("p (c n) -> p c n", c=8)
        WB = [f"w_bf{c}" for c in range(8)]
        wstage = sb("wstage", [128, 4096])
        w1_bf = [sb("w1k_bf", [128, 32, 128], BF16), sb("w1v_bf", [128, 32, 128], BF16)]
        w2_bf = sb("w2_bf", [128, 2, 64], BF16)
        w2_f = sb("w2_f", [128, 128])
        pe_f = sb("pe_f", [64, 64])
        pe_bf = sb("pe_bf", [64, 2, 32], BF16)
        b1 = sb("b1", [128, 2])
        gn = sb("gn_sb", [128, 8])
        cs = sb("cs_sb", [128, NS, 64])
        gk = sb("gk_sb", [128, 12, 64])
        gq = sb("gq_sb", [128, 16, 64])
        gkc = sb("gkc_sb", [128, 64])
        bfb = sb("bfb_sb", [128, 8])
        msel_f = sb("msel_f", [128, 2, 64])
        cval = sb("cval", [128, 2])
        f0 = sb("f0", [128, 2, 64])
        dmin = sb("dmin", [128, 2, 64])
        pt_sb = sb("pt_sb", [128, 256], I32)
        pio = sb("pio", [128, 1])
        tval = sb("tval", [128, 1])
        idx = sb("idx", [128, 256], I32)
        xt = sb("xt", [128, D])
        ss = sb("ss", [128, 1])
        rstd = sb("rstd", [128, 1])
        xs = sb("xs", [128, D], BF16)
        hT = sb("hT", [128, 8, 128], BF16)
        proj = sb("proj", [128, NIN])
        sq = sb("sq", [128, 16 * 64])
        ssh = sb("ssh", [128, 28])
        rsh = sb("rsh", [128, 28])
        rt = [sb(f"rt{i}", [128, 8, 32]) for i in range(4)]
        lf = sb("lf", [128, 8])
        lft = sb("lft", [128, 8])
        carry = sb("carry", [128, 8])
        c_all = sb("c_all", [128, 32, 8])
        c_seq = sb("c_seq", [128, 17, 8])
        kv_bf = sb("kv_bf", [128, 256], BF16)
        kcT = sb("kcT", [128, 4096], BF16)
        vcT = sb("vcT", [128, 4096], BF16)
        hsb = sb("hsb", [128, 256], BF16)
        kcn = sb("kcn", [128, 2, 64])
        kcn_bf = sb("kcn_bf", [128, 128], BF16)
        kcmpT = [sb(f"kcmpT{i}", [128, 256], BF16) for i in range(2)]
        vcaug = [sb(f"vcaug{i}", [128, 2, 2, 65], BF16) for i in range(2)]
        mselv = [sb(f"mselv{i}", [128, 2, 64], BF16) for i in range(2)]
        ones2 = sb("ones2", [128, 2])
        s4 = sb("s4", [128, 4])
        mEC = sb("mEC", [128, 8])
        qa2 = sb("qa2", [128, 512], BF16)
        qaT = sb("qaT", [128, 512], BF16)
        qb_bf = sb("qb_bf", [128, 512], BF16)
        qbT = sb("qbT", [128, 512], BF16)
        kvn = sb("kvn", [128, 256])
        kn_bf = sb("kn_bf", [128, 128], BF16)
        knT = sb("knT", [128, 128], BF16)
        vaugn = sb("vaugn", [128, 2, 65], BF16)
        kb_bf = sb("kb_bf", [128, 512], BF16)
        kbT = sb("kbT", [128, 512], BF16)
        vaugf = sb("vaugf", [128, 8, 65], BF16)
        PT = sb("PT", [128, 512], BF16)
        den = sb("den", [128, 8])
        obr = sb("obr", [128, 8, 64])
        gsig = sb("gsig", [128, 24])
        unsa = sb("unsa", [128, 8, 64])
        imp = sb("imp", [128, 64])
        impt = sb("impt", [128, 8, 64])
        imw = sb("imw", [128, 64])
        mx8 = sb("mx8", [128, 8])
        selb = sb("selb", [128, 2, 64], BF16)
        mk_tm = sb("mk_tm", [128, 128], BF16)
        mk = sb("mk", [128, 128], BF16)
        cref = sb("cref", [128, 8])
        bias = sb("bias", [128, 33, 8])
        lfs = sb("lfs", [128, 17, 8])
        zs = sb("zs", [128, 1024])
        aoff = [0]

        def carve(ncols_bf16):
            a = arena[:, aoff[0]:aoff[0] + ncols_bf16]
            aoff[0] += ncols_bf16
            return a
        w_out_bf = carve(8 * 1024).rearrange("p (c n) -> p c n", c=8)
        u_bf = carve(1024)
        uT = carve(1024)
        kvf = wstage[:, 0:1024]
        u = wstage[:, 1024:2048]
        ysb = wstage[:, 2048:3072]
        kvf2 = wstage[:, 3072:4096]
        PT2 = carve(512)
        kn_bf2 = carve(128)
        knT2 = carve(128)
        vaugn2 = carve(130).rearrange("p (a b) -> p a b", b=65)
        kb_bf2 = carve(512)
        kbT2 = carve(512)
        vaugf2 = carve(520).rearrange("p (a b) -> p a b", b=65)
        mk_tm2 = carve(128)
        mk2 = carve(128)
        mkall_tm = carve(4096)
        mk_all = carve(4096)
        ARENA_BUFS = ["w_out_bf", "u_bf", "uT", "PT1", "kn_bf1", "knT1", "vaugn1", "kb_bf1", "kbT1", "vaugf1", "mk_tm1", "mk1", "kvn1", "mkall_tm", "mk_all"]

        pT = ps("pT", [128, 1024], BF16)
        pP = [ps(f"pP{i}", [128, 512]) for i in range(2)]
        pS = ps("pS", [128, 512])
        acc = ps("acc", [128, 2, 512])
        pI = ps("pI", [128, 512])
        pC = ps("pC", [128, 16])

        V("pool", "memset", [], ["ident"], ap=ident[:], constant=0.0)
        V("pool", "affine_select", ["ident"], ["ident"], out=ident[:], in_=ident[:], pattern=[[-1, 128]],
          compare_op=ALU.not_equal, fill=1.0, base=0, channel_multiplier=1)
        V("pool", "memset", [], ["ones_f"], ap=ones_f[:], constant=1.0)
        V("pool", "memset", [], ["triU"], ap=triU[:], constant=1.0)
        V("pool", "affine_select", ["triU"], ["triU"], out=triU[:], in_=triU[:], pattern=[[1, 128]],
          compare_op=ALU.is_ge, fill=0.0, base=0, channel_multiplier=-1)
        V("pool", "memset", [], ["sel127"], ap=sel127[:], constant=1.0)
        V("pool", "affine_select", ["sel127"], ["sel127"], out=sel127[:], in_=sel127[:], pattern=[[0, 128]],
          compare_op=ALU.is_ge, fill=0.0, base=-127, channel_multiplier=1)
        V("pool", "memset", [], ["ones2"], ap=ones2[:], constant=1.0)
        V("pool", "memset", [], ["mEC"], ap=mEC[:], constant=1.0)
        V("pool", "affine_select", ["mEC"], ["mEC"], out=mEC[:, 0:2], in_=mEC[:, 0:2], pattern=[[-64, 2]], compare_op=ALU.is_ge, fill=0.0,
          base=-64, channel_multiplier=1)
        V("pool", "affine_select", ["mEC"], ["mEC"], out=mEC[:, 4:6], in_=mEC[:, 4:6], pattern=[[-64, 2]], compare_op=ALU.is_ge, fill=0.0,
          base=0, channel_multiplier=1)
        V("dve", "tensor_scalar", ["mEC"], ["mEC"], out=mEC[:, 2:4], in0=mEC[:, 0:2], scalar1=-1e4, scalar2=1e4, op0=ALU.mult, op1=ALU.add)
        V("dve", "tensor_scalar", ["mEC"], ["mEC"], out=mEC[:, 6:8], in0=mEC[:, 4:6], scalar1=-1.0, scalar2=None, op0=ALU.add)
        V("pool", "memset", [], ["carry"], ap=carry[:], constant=0.0)
        V("pool", "memset", [], ["zs"], ap=zs[:], constant=0.0)
        for (t_, i_, nm) in ((gn, gn_in, "gn"), (cs, cs_in, "cs"), (gk, gk_in, "gk"), (gq, gq_in, "gq"), (gkc, gkc_in, "gkc"),
                             (bfb, bf_in, "bfb"), (msel_f, msel_in, "msel_f"), (cval, cval_in, "cval"), (f0, f0_in, "f0"),
                             (dmin, dmin_in, "dmin"), (pt_sb, pt_in, "pt_sb"), (pio, pio_in, "pio"), (w2_f, w2_in, "w2_f"),
                             (pe_f, pe_in, "pe_f"), (tval, tval_in, "tval")):
            src = t_[:]
            if len(t_.shape) == 3:
                src = t_[:].rearrange("p a b -> p (a b)")
            dma("sp", src, i_, [], [nm], nm)
        V("dve", "tensor_copy", ["w2_f"], ["w2_bf"], out=w2_bf[:].rearrange("p a b -> p (a b)"), in_=w2_f[:])
        V("dve", "tensor_copy", ["pe_f"], ["pe_bf"], out=pe_bf[:].rearrange("p a b -> p (a b)"), in_=pe_f[:])
        V("dve", "tensor_scalar", ["pt_sb", "pio"], ["idx"], out=idx[:], in0=pt_sb[:], scalar1=128.0, scalar2=pio[:, 0:1],
          op0=ALU.mult, op1=ALU.add)
        for kv in range(2):
            dma("sp", wstage[:], w1_in[kv], [], ["wstage"], "wstage")
            V("dve", "tensor_copy", ["wstage"], [f"w1bf{kv}"], out=w1_bf[kv][:].rearrange("p a b -> p (a b)"), in_=wstage[:])

            def b1mm(e, kv=kv):
                ins = None
                for l in range(32):
                    ins = e.matmul(out=pC[:, kv:kv + 1], lhsT=w1_bf[kv][0:64, l, :], rhs=pe_bf[:, kv, l:l + 1], start=(l == 0), stop=(l == 31))
                return ins
            S.op("pe", b1mm, r=[f"w1bf{kv}", "pe_bf"], w=["pC"])
            V("dve", "tensor_copy", ["pC"], ["b1"], out=b1[:, kv:kv + 1], in_=pC[:, kv:kv + 1])
        for c in range(8):
            dma("sp", wstage[:, 0:NIN], w_in[c * 128:(c + 1) * 128, :], [], ["wstage"], "wstage")
            V("dve" if c % 2 == 0 else "pool", "tensor_scalar", ["wstage", "gn"], [WB[c]], out=w_bf[:, c, :], in0=wstage[:, 0:NIN],
              scalar1=gn[:, c:c + 1], scalar2=None, op0=ALU.mult)
        for q_i in range(16):
            dma("act", o_wins[q_i, 0:504, :], win_in[q_i, 8:512, :], [], [f"owins{q_i}"], f"owins{q_i % 4}")

        def transposes(in_aps, r, w_out_ap, wname, eng="act"):
            n = len(in_aps)

            def tr(e):
                ins = None
                for k, a in enumerate(in_aps):
                    ins = e.transpose(out=pT[:, k * 128:(k + 1) * 128], in_=a, identity=ident[:])
                return ins
            S.op("pe", tr, r=list(r) + ["ident"], w=["pT"])
            if eng == "act":
                V("act", "copy", ["pT"], [wname], out=w_out_ap, in_=pT[:, 0:n * 128])
            else:
                V("dve", "tensor_copy", ["pT"], [wname], out=w_out_ap, in_=pT[:, 0:n * 128])

        def head_norm(reg, nh, hoff, gt, gname, R, W):
            V("dve", "tensor_tensor", R, ["sq"], out=sq[:, 0:nh * 64], in0=reg, in1=reg, op=ALU.mult)
            V("dve", "tensor_reduce", ["sq"], ["ssh"], out=ssh[:, hoff:hoff + nh], in_=sq[:, 0:nh * 64].rearrange("p (h d) -> p h d", d=64),
              axis=AX.X, op=ALU.add)
            V("dve", "tensor_scalar", ["ssh"], ["rsh"], out=rsh[:, hoff:hoff + nh], in0=ssh[:, hoff:hoff + nh], scalar1=1.0 / 64,
              scalar2=EPS, op0=ALU.mult, op1=ALU.add)
            V("act", "activation", ["rsh"], ["rsh"], out=rsh[:, hoff:hoff + nh], in_=rsh[:, hoff:hoff + nh], func=AF.Sqrt)
            V("dve", "reciprocal", ["rsh"], ["rsh"], out=rsh[:, hoff:hoff + nh], in_=rsh[:, hoff:hoff + nh])
            r3 = reg.rearrange("p (h d) -> p h d", d=64)
            V("dve", "tensor_tensor", R + ["rsh"], W, out=r3, in0=r3,
              in1=rsh[:, hoff:hoff + nh].unsqueeze(2).to_broadcast([128, nh, 64]), op=ALU.mult)
            V("pool", "tensor_tensor", R + [gname], W, out=r3, in0=r3, in1=gt, op=ALU.mult)

        PRK = [f"projc{ci}" for ci in range(4)]
        PRQ = [f"projc{ci}" for ci in range(4, 9)]
        for t in range(NS):
            own = (t % 2 == 1) or (t == 32)
            oj = 16 if t == 32 else (t - 1) // 2
            rows = slice(t * 128, (t + 1) * 128)
            dma("sp", xt[:], x_in[rows, :], [], ["xt"], "xt")
            V("act", "activation", ["xt"], ["xs", "ss"], out=xs[:], in_=xt[:], func=AF.Square, accum_out=ss[:])
            V("dve", "tensor_scalar", ["ss"], ["rstd"], out=rstd[:], in0=ss[:], scalar1=1.0 / D, scalar2=EPS, op0=ALU.mult, op1=ALU.add)
            V("act", "activation", ["rstd"], ["rstd"], out=rstd[:], in_=rstd[:], func=AF.Sqrt)
            V("dve", "reciprocal", ["rstd"], ["rstd"], out=rstd[:], in_=rstd[:])
            V("dve", "tensor_scalar", ["xt", "rstd"], ["xs"], out=xs[:], in0=xt[:], scalar1=rstd[:, 0:1], scalar2=None, op0=ALU.mult)
            transposes([xs[:, c * 128:(c + 1) * 128] for c in range(8)], ["xs"], hT[:].rearrange("p a b -> p (a b)"), "hT")
            nch = 9 if own else 4
            for ci in range(nch):
                off, wd = CHUNKS[ci]
                pb = ci % 2

                def mm(e, off=off, wd=wd, pb=pb):
                    ins = None
                    for c in range(8):
                        ins = e.matmul(out=pP[pb][:, 0:wd], lhsT=hT[:, c, :], rhs=w_bf[:, c, off:off + wd], start=(c == 0), stop=(c == 7))
                    return ins
                S.op("pe", mm, r=["hT"] + WB, w=[f"pP{pb}"])
                if ci % 2 == 0:
                    V("act", "copy", [f"pP{pb}"], [f"projc{ci}"], out=proj[:, off:off + wd], in_=pP[pb][:, 0:wd])
                else:
                    V("dve", "tensor_copy", [f"pP{pb}"], [f"projc{ci}"], out=proj[:, off:off + wd], in_=pP[pb][:, 0:wd])
            head_norm(proj[:, O_KS:O_KS + 768], 12, 0, gk[:], "gk", PRK, PRK)
            if own:
                head_norm(proj[:, O_QA:O_QA + 1024], 16, 12, gq[:], "gq", PRQ, PRQ)
            for (off, nh, eng, PR) in ((O_KC, 6, "dve", PRK), (O_QA, 8, "pool", PRQ)):
                if off == O_QA and not own:
                    continue
                r4 = proj[:, off:off + nh * 64].rearrange("p (h two d) -> p h two d", two=2, d=32)
                x1, x2 = r4[:, :, 0, :], r4[:, :, 1, :]
                cosb = cs[:, t, 0:32].unsqueeze(1).to_broadcast([128, nh, 32])
                sinb = cs[:, t, 32:64].unsqueeze(1).to_broadcast([128, nh, 32])
                k = 0 if eng == "dve" else 2
                ta, tb = rt[k][:, 0:nh, :], rt[k + 1][:, 0:nh, :]
                RT = [f"rt{k}", f"rt{k + 1}"]
                V(eng, "tensor_tensor", PR + ["cs"], [RT[0]], out=ta, in0=x1, in1=sinb, op=ALU.mult)
                V(eng, "tensor_tensor", PR + ["cs"], [RT[1]], out=tb, in0=x2, in1=sinb, op=ALU.mult)
                V(eng, "tensor_tensor", PR + ["cs"], PR, out=x1, in0=x1, in1=cosb, op=ALU.mult)
                V(eng, "tensor_tensor", PR + ["cs"], PR, out=x2, in0=x2, in1=cosb, op=ALU.mult)
                V(eng, "tensor_tensor", PR + RT, PR, out=x1, in0=x1, in1=tb, op=ALU.subtract)
                V(eng, "tensor_tensor", PR + RT, PR, out=x2, in0=x2, in1=ta, op=ALU.add)
            V("dve", "tensor_tensor", PRK + ["bfb"], ["lft"], out=lft[:], in0=proj[:, O_FB:O_FB + 8], in1=bfb[:], op=ALU.add)
            V("act", "activation", ["lft"], ["lft"], out=lft[:], in_=lft[:], func=AF.Exp, scale=-1.0)
            V("act", "activation", ["lft"], ["lft"], out=lft[:], in_=lft[:], func=AF.Ln, bias=1.0)
            V("dve", "tensor_scalar", ["lft"], ["lf"], out=lf[:], in0=lft[:], scalar1=-1.0, scalar2=None, op0=ALU.mult)
            orow = slice(oj * 128, (oj + 1) * 128)
            for (oap, sap, ka, va, wdt, nm) in ((o_cmp, sc_cmp, O_KC, O_VC, 128, "cmp"), (o_slc, sc_slc, O_KS, O_VS, 128, "slc"),
                                                (o_win, sc_win, O_KW, O_VW, 128, "win"), (o_fox, sc_fox, O_KB, O_VB, 512, "fox")):
                dma("act", sap[rows, 0:wdt], proj[:, ka:ka + wdt], PRK, [f"sc{nm}{t}"], f"stsc{t % 8}")
                dma("act", sap[rows, wdt:2 * wdt], proj[:, va:va + wdt], PRK, [f"sc{nm}{t}"], f"stsc{t % 8}")
                if own:
                    dma("act", oap[orow, 0:wdt], proj[:, ka:ka + wdt], PRK, [f"o{nm}k{t}"], "st")
                    dma("act", oap[orow, wdt:2 * wdt], proj[:, va:va + wdt], PRK, [f"o{nm}v{t}"], "st")
            if own:
                dma("act", o_lf[orow, :], lf[:], ["lf"], [f"olf{t}"], "st")
                dma("act", qstore[orow, :], proj[:, O_QA:NIN], PRQ, [f"qst{oj}"], f"stq{oj % 4}")
            if t == 32:
                dma("act", sc_lf, lf[:], ["lf"], ["sc_lf"], "stsc")
                for q_i in range(16):
                    dma("act", o_wins[q_i, 504:512, 0:128], proj[q_i * 8:(q_i + 1) * 8, O_KW:O_KW + 128], PRK + [f"owins{q_i}"],
                        [f"owinsn{q_i}"], "st")
                    dma("act", o_wins[q_i, 504:512, 128:256], proj[q_i * 8:(q_i + 1) * 8, O_VW:O_VW + 128], PRK + [f"owins{q_i}"],
                        [f"owinsm{q_i}"], "st")
            else:
                def cmm(e):
                    e.matmul(out=pC[:, 0:8], lhsT=triU[:], rhs=lf[:], start=True, stop=True)
                    return e.matmul(out=pC[:, 8:16], lhsT=ones_f[:], rhs=lf[:], start=True, stop=True)
                S.op("pe", cmm, r=["lf", "triU", "ones_f"], w=["pC"])
                V("dve", "tensor_tensor", ["pC", "carry"], [f"c_all{t}"], out=c_all[:, t, :], in0=pC[:, 0:8], in1=carry[:], op=ALU.add)
                V("dve", "tensor_tensor", ["pC", "carry"], ["carry"], out=carry[:], in0=pC[:, 8:16], in1=carry[:], op=ALU.add)
                V("dve", "tensor_copy", PRK, ["kv_bf"], out=kv_bf[:, 0:128], in_=proj[:, O_KC:O_KC + 128])
                V("dve", "tensor_copy", PRK, ["kv_bf"], out=kv_bf[:, 128:256], in_=proj[:, O_VC:O_VC + 128])

                def tr2(e):
                    e.transpose(out=pT[:, 0:128], in_=kv_bf[:, 0:128], identity=ident[:])
                    return e.transpose(out=pT[:, 128:256], in_=kv_bf[:, 128:256], identity=ident[:])
                S.op("pe", tr2, r=["kv_bf", "ident"], w=["pT"])
                V("act", "copy", ["pT"], ["kcT"], out=kcT[:, t * 128:(t + 1) * 128], in_=pT[:, 0:128])
                V("act", "copy", ["pT"], ["vcT"], out=vcT[:, t * 128:(t + 1) * 128], in_=pT[:, 128:256])

        AW = ARENA_BUFS + WB
        dma("sp", wstage[:, 0:1024], w_out_in[0:128, :], [], ["wstage"], "wstage")
        for c in range(8):
            V("dve", "tensor_copy", ["wstage"], AW if c == 0 else ["w_out_bf"], out=w_out_bf[:, c, :], in_=wstage[:, 0:1024])
            if c < 7:
                dma("sp", wstage[:, 0:1024], w_out_in[(c + 1) * 128:(c + 2) * 128, :], [], ["wstage"], "wstage")

        def compress(ncmp, slot, use_cval):
            nt_n = (ncmp + 127) // 128
            V("pool", "memset", [], [f"vcaug{slot}"], ap=vcaug[slot][:].rearrange("p a b c -> p (a b c)"), constant=0.0)
            V("pool", "memset", [], [f"kcmpT{slot}"], ap=kcmpT[slot][:], constant=0.0)
            for kv in range(2):
                src, sname = (kcT, "kcT") if kv == 0 else (vcT, "vcT")
                for g in range(2):
                    def hmm(e, kv=kv, g=g, src=src):
                        ins = None
                        for l in range(32):
                            ins = e.matmul(out=pP[0][:, 0:ncmp], lhsT=w1_bf[kv][64 * g:64 * g + 64, l, :],
                                           rhs=src[64 * g:64 * g + 64, l:l + 16 * (ncmp - 1) + 1:16], start=(l == 0), stop=(l == 31))
                        return ins
                    S.op("pe", hmm, r=[sname, f"w1bf{kv}"], w=["pP0"])
                    V("act", "activation", ["pP0", "b1"], ["hsb"], out=hsb[:, 0:ncmp], in_=pP[0][:, 0:ncmp], func=AF.Silu, bias=b1[:, kv:kv + 1])
                    for nt in range(nt_n):
                        n0 = nt * 128
                        nn = min(128, ncmp - n0)
                        S.op("pe", (lambda nt, n0, nn, kv: lambda e: e.matmul(out=pP[1][0:nn, nt * 64:nt * 64 + 64], lhsT=hsb[:, n0:n0 + nn],
                                                                               rhs=w2_bf[:, kv, :], start=True, stop=True))(nt, n0, nn, kv),
                             r=["hsb", "w2_bf"], w=["pP1"])
                        if kv == 0:
                            V("dve", "tensor_copy", ["pP1"], ["kcn"], out=kcn[0:nn, nt, :], in_=pP[1][0:nn, nt * 64:nt * 64 + 64])
                        else:
                            cv = cval[0:nn, nt:nt + 1] if use_cval else ones2[0:nn, 0:1]
                            V("dve", "tensor_scalar", ["pP1", "cval", "ones2"], [f"vcaug{slot}"], out=vcaug[slot][0:nn, nt, g, 0:64],
                              in0=pP[1][0:nn, nt * 64:nt * 64 + 64], scalar1=cv, scalar2=None, op0=ALU.mult)
                            V("dve", "tensor_copy", ["cval", "ones2"], [f"vcaug{slot}"], out=vcaug[slot][0:nn, nt, g, 64:65], in_=cv)
                    if kv == 0:
                        for nt in range(nt_n):
                            nn = min(128, ncmp - nt * 128)
                            V("dve", "tensor_tensor", ["kcn"], ["sq"], out=sq[0:nn, 0:64], in0=kcn[0:nn, nt, :], in1=kcn[0:nn, nt, :], op=ALU.mult)
                            V("dve", "tensor_reduce", ["sq"], ["s4"], out=s4[0:nn, 0:1], in_=sq[0:nn, 0:64], axis=AX.X, op=ALU.add)
                            V("dve", "tensor_scalar", ["s4"], ["s4"], out=s4[0:nn, 1:2], in0=s4[0:nn, 0:1], scalar1=1.0 / 64, scalar2=EPS,
                              op0=ALU.mult, op1=ALU.add)
                            V("act", "activation", ["s4"], ["s4"], out=s4[0:nn, 2:3], in_=s4[0:nn, 1:2], func=AF.Sqrt)
                            V("dve", "reciprocal", ["s4"], ["s4"], out=s4[0:nn, 3:4], in_=s4[0:nn, 2:3])
                            V("dve", "tensor_scalar", ["kcn", "s4"], ["kcn"], out=kcn[0:nn, nt, :], in0=kcn[0:nn, nt, :], scalar1=s4[0:nn, 3:4],
                              scalar2=None, op0=ALU.mult)
                            V("pool", "memset", [], ["kcn_bf"], ap=kcn_bf[:, 64 * g:64 * g + 64], constant=0.0)
                            V("pool", "tensor_tensor", ["kcn", "gkc"], ["kcn_bf"], out=kcn_bf[0:nn, 64 * g:64 * g + 64], in0=kcn[0:nn, nt, :],
                              in1=gkc[0:nn, :], op=ALU.mult)
                            S.op("pe", lambda e: e.transpose(out=pT[:, 0:128], in_=kcn_bf[:], identity=ident[:]), r=["kcn_bf", "ident"], w=["pT"])
                            V("act", "copy", ["pT"], [f"kcmpT{slot}"], out=kcmpT[slot][64 * g:64 * g + 64, nt * 128:(nt + 1) * 128],
                              in_=pT[64 * g:64 * g + 64, 0:128])
            for nt in range(2):
                cv = cval[:, nt:nt + 1] if use_cval else ones2[:, 0:1]
                V("dve", "tensor_scalar", ["msel_f", "cval", "ones2"], [f"mselv{slot}"], out=mselv[slot][:, nt, :], in0=msel_f[:, nt, :],
                  scalar1=cv, scalar2=None, op0=ALU.mult)

        def sel_mask(eng_r, buf, ap, pattern, base, cm, fill=0.0):
            V("pool", "affine_select", eng_r + [buf], [buf], out=ap, in_=ap, pattern=pattern, compare_op=ALU.is_ge, fill=fill,
              base=base, channel_multiplier=cm)

        def norm_evac(dst3, nheads_total=8):
            for g in range(2):
                a3 = acc[:, g, 0:260].rearrange("p (r e) -> p r e", e=65)
                V("dve", "tensor_scalar", ["acc"], ["den"], out=den[:, 4 * g:4 * g + 4], in0=a3[:, :, 64], scalar1=1e-30, scalar2=None, op0=ALU.max)
                V("dve", "reciprocal", ["den"], ["den"], out=den[:, 4 * g:4 * g + 4], in_=den[:, 4 * g:4 * g + 4])
                V("dve", "tensor_tensor", ["acc", "den"], ["obr"], out=dst3[:, 4 * g:4 * g + 4, :], in0=a3[:, :, 0:64],
                  in1=den[:, 4 * g:4 * g + 4].unsqueeze(2).to_broadcast([128, 4, 64]), op=ALU.mult)

        kvn2 = carve(512).bitcast(F32)
        B_PT = [(PT, "PT"), (PT2, "PT1")]
        B_PS = [(pS, "pS"), (pP[1], "pP1")]
        B_KVN = [(kvn, "kvn"), (kvn2, "kvn1")]
        B_KNB = [(kn_bf, "kn_bf"), (kn_bf2, "kn_bf1")]
        B_KNT = [(knT, "knT"), (knT2, "knT1")]
        B_VAN = [(vaugn, "vaugn"), (vaugn2, "vaugn1")]
        B_KVF = [(kvf, "kvf"), (kvf2, "kvf1")]
        B_KBB = [(kb_bf, "kb_bf"), (kb_bf2, "kb_bf1")]
        B_KBT = [(kbT, "kbT"), (kbT2, "kbT1")]
        B_VAF = [(vaugf, "vaugf"), (vaugf2, "vaugf1")]
        B_MKT = [(mk_tm, "mk_tm"), (mk_tm2, "mk_tm1")]
        B_MK = [(mk, "mk"), (mk2, "mk1")]

        def ones_col(ap, name, kt, tv):
            V("pool", "memset", [], [name], ap=ap, constant=1.0)
            if kt == 0 and tv:
                V("dve", "tensor_scalar", ["tval", name], [name], out=ap, in0=ap, scalar1=tval[:, 0:1], scalar2=None, op0=ALU.mult)

        def attn(nk, s0, load_nsa, load_fox, cT, cslot, fslot, qload, xload, tv):
            kd = nk - 1
            PRQ_ = ["projq"]
            qload()
            R_ = PRQ_
            V("act", "activation", R_, ["gsig"], out=gsig[:], in_=proj[:, O_GA:O_GA + 24], func=AF.Exp, scale=-1.0)
            V("dve", "tensor_scalar", ["gsig"], ["gsig"], out=gsig[:], in0=gsig[:], scalar1=1.0, scalar2=None, op0=ALU.add)
            V("dve", "reciprocal", ["gsig"], ["gsig"], out=gsig[:], in_=gsig[:])
            V("act", "activation", R_, ["zs"], out=zs[:], in_=proj[:, O_ZA:O_ZA + 1024], func=AF.Silu)
            V("dve", "tensor_copy", R_, ["qa2"], out=qa2[:].rearrange("p (r g d) -> p r g d", r=4, g=2),
              in_=proj[:, O_QA:O_QA + 512].rearrange("p (g r d) -> p r g d", g=2, r=4))
            transposes([qa2[:, r * 128:(r + 1) * 128] for r in range(4)], ["qa2"], qaT[:], "qaT")
            V("dve", "tensor_copy", R_, ["qb_bf"], out=qb_bf[:], in_=proj[:, O_QB:O_QB + 512])
            transposes([qb_bf[:, r * 128:(r + 1) * 128] for r in range(4)], ["qb_bf"], qbT[:], "qbT")

            if STOP <= 0:
                return
            for g in range(2):
                for nt in range(2):
                    (PT_, PTn), (pS_, pSn) = B_PT[nt], B_PS[nt]
                    S.op("pe", (lambda g, nt, pS_: lambda e: e.matmul(out=pS_[:], lhsT=kcmpT[cslot][64 * g:64 * g + 64, nt * 128:(nt + 1) * 128],
                                                                 rhs=qaT[64 * g:64 * g + 64, :], start=True, stop=True))(g, nt, pS_),
                         r=[f"kcmpT{cslot}", "qaT"], w=[pSn])
                    V("act", "activation", [pSn], [PTn], out=PT_[:], in_=pS_[:], func=AF.Exp, scale=0.125)
                    vis_all = (s0 - 31 - 2048 * nt - 16 * 127) >= 0
                    vis_none = (s0 + 127 - 31 - 2048 * nt) < 0
                    if vis_none:
                        V("pool", "memset", [], [PTn], ap=PT_[:], constant=0.0)
                    elif not vis_all:
                        sel_mask([], PTn, PT_[:].rearrange("p (r q) -> p r q", r=4), [[0, 4], [1, 128]], s0 - 31 - 2048 * nt, -16)

                    def pv(e, g=g, nt=nt, PT=PT_):
                        ins = None
                        for r in range(4):
                            e.matmul(out=acc[:, g, r * 65:(r + 1) * 65], lhsT=PT[:, r * 128:(r + 1) * 128], rhs=vcaug[cslot][:, nt, g, :],
                                     start=(nt == 0 and r == 0), stop=(nt == 1 and r == 3))
                            ins = e.matmul(out=pI[:, (4 * g + r) * 64:(4 * g + r + 1) * 64], lhsT=PT[:, r * 128:(r + 1) * 128],
                                           rhs=mselv[cslot][:, nt, :], start=(g == 0 and nt == 0 and r == 0), stop=(g == 1 and nt == 1 and r == 3))
                        return ins
                    S.op("pe", pv, r=[PTn, f"vcaug{cslot}", f"mselv{cslot}"], w=["acc", "pI"])
            norm_evac(obr)
            V("dve", "tensor_tensor", ["obr", "gsig"], ["unsa"], out=unsa[:], in0=obr[:],
              in1=gsig[:].rearrange("p (h c) -> p h c", c=3)[:, :, 0:1].to_broadcast([128, 8, 64]), op=ALU.mult)
            if STOP <= 1:
                return
            V("dve", "tensor_tensor", ["pI", "den"], ["impt"], out=impt[:], in0=pI[:].rearrange("p (h j) -> p h j", j=64),
              in1=den[:].unsqueeze(2).to_broadcast([128, 8, 64]), op=ALU.mult)
            for g in range(2):
                V("dve", "tensor_tensor", ["impt"], ["imp"], out=imp[:], in0=impt[:, 4 * g, :], in1=impt[:, 4 * g + 1, :], op=ALU.add)
                V("dve", "tensor_tensor", ["impt", "imp"], ["imp"], out=imp[:], in0=imp[:], in1=impt[:, 4 * g + 2, :], op=ALU.add)
                V("dve", "tensor_tensor", ["impt", "imp"], ["imp"], out=imp[:], in0=imp[:], in1=impt[:, 4 * g + 3, :], op=ALU.add)
                V("dve", "tensor_tensor", ["imp", "dmin"], ["imp"], out=imp[:], in0=imp[:], in1=dmin[:, fslot, :], op=ALU.min)
                b0 = s0 // 64
                xx = imp[:, b0:b0 + 2]
                V("dve", "tensor_tensor", ["imp", "mEC"], ["imp"], out=xx, in0=xx, in1=mEC[:, 0:2], op=ALU.mult)
                V("dve", "tensor_tensor", ["imp", "mEC"], ["imp"], out=xx, in0=xx, in1=mEC[:, 2:4], op=ALU.add)
                V("dve", "tensor_tensor", ["imp", "mEC"], ["imp"], out=xx, in0=xx, in1=mEC[:, 4:6], op=ALU.mult)
                V("dve", "tensor_tensor", ["imp", "mEC"], ["imp"], out=xx, in0=xx, in1=mEC[:, 6:8], op=ALU.add)
                if b0 + 2 < 64:
                    V("pool", "memset", ["imp"], ["imp"], ap=imp[:, b0 + 2:64], constant=-1.0)
                V("dve", "tensor_tensor", ["imp", "f0"], ["imp"], out=imp[:], in0=imp[:], in1=f0[:, fslot, :], op=ALU.max)
                V("dve", "max", ["imp"], ["mx8"], out=mx8[:], in_=imp[:])
                V("dve", "match_replace", ["imp", "mx8"], ["imw"], out=imw[:], in_to_replace=mx8[:], in_values=imp[:], imm_value=-3.0)
                V("dve", "max", ["imw"], ["mx8"], out=mx8[:], in_=imw[:])
                V("dve", "tensor_scalar", ["imp", "mx8"], ["imw"], out=imw[:], in0=imp[:], scalar1=mx8[:, 7:8], scalar2=None, op0=ALU.is_ge)
                V("dve", "tensor_copy", ["imw"], ["selb"], out=selb[:, 0, :], in_=imw[:])
                if STOP <= 2:
                    continue
                V("dve", "tensor_copy", ["selb"], ["mkall_tm"], out=mkall_tm[:, 0:nk * 128].rearrange("p (a b) -> p a b", b=64),
                  in_=selb[:, 0, 0:2 * nk].unsqueeze(2).to_broadcast([128, 2 * nk, 64]))
                for k0_ in range(0, nk, 8):
                    k1_ = min(nk, k0_ + 8)
                    transposes([mkall_tm[:, k * 128:(k + 1) * 128] for k in range(k0_, k1_)], ["mkall_tm"], mk_all[:, k0_ * 128:k1_ * 128], "mk_all")
                for kt in range(nk):
                    pp = kt % 2
                    (PT_, PTn), (pS_, pSn), (kvn_, kvnn), (knb_, knbn), (knT_, knTn), (van_, vann), (mkt_, mktn), (mk_, mkn) = (
                        B_PT[pp], B_PS[pp], B_KVN[pp], B_KNB[pp], B_KNT[pp], B_VAN[pp], B_MKT[pp], B_MK[pp])
                    load_nsa("slc", kt, kvn_, kvnn)
                    V("dve", "tensor_copy", [kvnn], [knbn], out=knb_[:], in_=kvn_[:, 0:128])
                    transposes([knb_[:]], [knbn], knT_[:], knTn)
                    ones_col(van_[:, :, 64:65], vann, kt, tv)
                    V("pool", "tensor_copy", [kvnn], [vann], out=van_[:, :, 0:64], in_=kvn_[:, 128:256].rearrange("p (g d) -> p g d", g=2))
                    S.op("pe", (lambda g, pS_, knT_: lambda e: e.matmul(out=pS_[:], lhsT=knT_[64 * g:64 * g + 64, :], rhs=qaT[64 * g:64 * g + 64, :],
                                                                        start=True, stop=True))(g, pS_, knT_), r=[knTn, "qaT"], w=[pSn])
                    V("act", "activation", [pSn], [PTn], out=PT_[:], in_=pS_[:], func=AF.Exp, scale=0.125)
                    V("dve", "tensor_tensor", [PTn, "mk_all"], [PTn], out=PT_[:].rearrange("p (r q) -> p r q", r=4),
                      in0=PT_[:].rearrange("p (r q) -> p r q", r=4),
                      in1=mk_all[:, kt * 128:(kt + 1) * 128].unsqueeze(1).to_broadcast([128, 4, 128]), op=ALU.mult)
                    if kt == kd:
                        sel_mask([], PTn, PT_[:].rearrange("p (r q) -> p r q", r=4), [[0, 4], [1, 128]], 0, -1)

                    def pv2(e, g=g, kt=kt, PT=PT_, vaugn=van_):
                        ins = None
                        for r in range(4):
                            ins = e.matmul(out=acc[:, g, r * 65:(r + 1) * 65], lhsT=PT[:, r * 128:(r + 1) * 128], rhs=vaugn[:, g, :],
                                           start=(kt == 0 and r == 0), stop=(kt == nk - 1 and r == 3))
                        return ins
                    S.op("pe", pv2, r=[PTn, vann], w=["acc"])
            norm_evac(obr)
            V("dve", "tensor_tensor", ["obr", "gsig"], ["obr"], out=obr[:], in0=obr[:],
              in1=gsig[:].rearrange("p (h c) -> p h c", c=3)[:, :, 1:2].to_broadcast([128, 8, 64]), op=ALU.mult)
            V("dve", "tensor_tensor", ["obr", "unsa"], ["unsa"], out=unsa[:], in0=unsa[:], in1=obr[:], op=ALU.add)
            if STOP <= 3:
                return
            k0 = max(0, nk - 5)
            for kt in range(k0, nk):
                pp = kt % 2
                (kvn_, kvnn), (knb_, knbn), (knT_, knTn), (van_, vann) = B_KVN[pp], B_KNB[pp], B_KNT[pp], B_VAN[pp]
                load_nsa("win", kt, kvn_, kvnn)
                V("dve", "tensor_copy", [kvnn], [knbn], out=knb_[:], in_=kvn_[:, 0:128])
                transposes([knb_[:]], [knbn], knT_[:], knTn)
                ones_col(van_[:, :, 64:65], vann, kt, tv)
                V("pool", "tensor_copy", [kvnn], [vann], out=van_[:, :, 0:64], in_=kvn_[:, 128:256].rearrange("p (g d) -> p g d", g=2))
                for g in range(2):
                    (PT_, PTn), (pS_, pSn) = B_PT[g], B_PS[g]
                    S.op("pe", (lambda g, pS_, knT_: lambda e: e.matmul(out=pS_[:], lhsT=knT_[64 * g:64 * g + 64, :], rhs=qaT[64 * g:64 * g + 64, :],
                                                                        start=True, stop=True))(g, pS_, knT_), r=[knTn, "qaT"], w=[pSn])
                    V("act", "activation", [pSn], [PTn], out=PT_[:], in_=pS_[:], func=AF.Exp, scale=0.125)
                    if kt == kd:
                        sel_mask([], PTn, PT_[:].rearrange("p (r q) -> p r q", r=4), [[0, 4], [1, 128]], 0, -1)
                    if kt == nk - 5:
                        sel_mask([], PTn, PT_[:].rearrange("p (r q) -> p r q", r=4), [[0, 4], [-1, 128]], 0, 1)

                    def pv3(e, g=g, kt=kt, PT=PT_, vaugn=van_):
                        ins = None
                        for r in range(4):
                            ins = e.matmul(out=acc[:, g, r * 65:(r + 1) * 65], lhsT=PT[:, r * 128:(r + 1) * 128], rhs=vaugn[:, g, :],
                                           start=(kt == k0 and r == 0), stop=(kt == nk - 1 and r == 3))
                        return ins
                    S.op("pe", pv3, r=[PTn, vann], w=["acc"])
            norm_evac(obr)
            V("dve", "tensor_tensor", ["obr", "gsig"], ["obr"], out=obr[:], in0=obr[:],
              in1=gsig[:].rearrange("p (h c) -> p h c", c=3)[:, :, 2:3].to_broadcast([128, 8, 64]), op=ALU.mult)
            V("dve", "tensor_tensor", ["obr", "unsa"], ["unsa"], out=unsa[:], in0=unsa[:], in1=obr[:], op=ALU.add)
            V("dve", "tensor_tensor", ["unsa", "zs"], ["u"], out=u[:, 0:512], in0=unsa[:].rearrange("p h d -> p (h d)"), in1=zs[:, 0:512], op=ALU.mult)
            if STOP <= 4:
                return
            S.op("pe", lambda e: e.matmul(out=pC[:, 0:8], lhsT=sel127[:], rhs=cT[:, kd, :], start=True, stop=True), r=["cT", "sel127"], w=["pC"])
            V("dve", "tensor_copy", ["pC"], ["cref"], out=cref[:], in_=pC[:, 0:8])
            V("dve", "tensor_tensor", ["cref", "cT"], ["bias"], out=bias[:, 0:nk, :], in0=cref[:].unsqueeze(1).to_broadcast([128, nk, 8]),
              in1=cT[:, 0:nk, :], op=ALU.subtract)
            for kt in range(nk):
                pp = kt % 2
                (kvf_, kvfn), (kbb_, kbbn), (kbT_, kbTn), (vaf_, vafn) = B_KVF[pp], B_KBB[pp], B_KBT[pp], B_VAF[pp]
                load_fox(kt, kvf_, kvfn)
                V("dve", "tensor_copy", [kvfn], [kbbn], out=kbb_[:], in_=kvf_[:, 0:512])
                transposes([kbb_[:, r * 128:(r + 1) * 128] for r in range(4)], [kbbn], kbT_[:], kbTn)
                ones_col(vaf_[:, :, 64:65], vafn, kt, tv)
                V("pool", "tensor_copy", [kvfn], [vafn], out=vaf_[:, :, 0:64], in_=kvf_[:, 512:1024].rearrange("p (h d) -> p h d", h=8))
                for hp in range(2):
                    (PT_, PTn), (pS_, pSn) = B_PT[hp], B_PS[hp]
                    for hh in range(4):
                        h = 4 * hp + hh
                        pr, lo = h // 2, 64 * (h % 2)
                        S.op("pe", (lambda hh, pr, lo, pS_, kbT_: lambda e: e.matmul(out=pS_[:, hh * 128:(hh + 1) * 128],
                                                                                     lhsT=kbT_[lo:lo + 64, pr * 128:(pr + 1) * 128],
                                                                                     rhs=qbT[lo:lo + 64, pr * 128:(pr + 1) * 128],
                                                                                     start=True, stop=True))(hh, pr, lo, pS_, kbT_),
                             r=[kbTn, "qbT"], w=[pSn])
                    for hh in range(4):
                        h = 4 * hp + hh
                        V("act", "activation", [pSn, "bias"], [PTn], out=PT_[:, hh * 128:(hh + 1) * 128], in_=pS_[:, hh * 128:(hh + 1) * 128],
                          func=AF.Exp, scale=0.125, bias=bias[:, kt, h:h + 1])
                    if kt == kd:
                        sel_mask([], PTn, PT_[:].rearrange("p (r q) -> p r q", r=4), [[0, 4], [1, 128]], 0, -1)

                    def pv4(e, hp=hp, kt=kt, PT=PT_, vaugf=vaf_):
                        ins = None
                        for hh in range(4):
                            ins = e.matmul(out=acc[:, hp, hh * 65:(hh + 1) * 65], lhsT=PT[:, hh * 128:(hh + 1) * 128], rhs=vaugf[:, 4 * hp + hh, :],
                                           start=(kt == 0 and hh == 0), stop=(kt == nk - 1 and hh == 3))
                        return ins
                    S.op("pe", pv4, r=[PTn, vafn], w=["acc"])
            norm_evac(obr)
            V("dve", "tensor_tensor", ["obr", "zs"], ["u"], out=u[:, 512:1024], in0=obr[:].rearrange("p h d -> p (h d)"), in1=zs[:, 512:1024], op=ALU.mult)
            if STOP <= 5:
                return
            V("dve", "tensor_copy", ["u"], ["u_bf"], out=u_bf[:], in_=u[:])
            transposes([u_bf[:, c * 128:(c + 1) * 128] for c in range(8)], ["u_bf"], uT[:], "uT")
            xload()
            for hc in range(2):
                def omm(e, hc=hc):
                    ins = None
                    for c in range(8):
                        ins = e.matmul(out=pP[hc][:], lhsT=uT[:, c * 128:(c + 1) * 128], rhs=w_out_bf[:, c, hc * 512:(hc + 1) * 512],
                                       start=(c == 0), stop=(c == 7))
                    return ins
                S.op("pe", omm, r=["uT", "w_out_bf"], w=[f"pP{hc}"])
                V("dve", "tensor_tensor", [f"pP{hc}", "xt"], ["ysb"], out=ysb[:, hc * 512:(hc + 1) * 512], in0=pP[hc][:], in1=xt[:, hc * 512:(hc + 1) * 512], op=ALU.add)

        compress(255, 0, True)

        def p_load_nsa(which, kt, dst, dn):
            src = sc_slc if which == "slc" else sc_win
            dma("sp", dst[:], src[kt * 128:(kt + 1) * 128, :], [f"sc{which}{kt}"], [dn], dn)

        def p_load_fox(kt, dst, dn):
            dma("sp", dst[:], sc_fox[kt * 128:(kt + 1) * 128, :], [f"scfox{kt}"], [dn, "wstage"], dn)
        S.lastw["cT"] = S.lastw["c_all31"]
        S.readers["cT"] = []
        for j in range(npq):
            orow = slice(j * 128, (j + 1) * 128)

            def qload(orow=orow, j=j):
                dma("sp", proj[:, O_QA:NIN], qstore[orow, :], [f"qst{j}"], PRK + PRQ + ["projq"], "projq")

            def xload(j=j):
                xr = 2 * j + 1
                dma("sp", xt[:], x_in[xr * 128:(xr + 1) * 128, :], [], ["xt"], "xt")
            attn(2 * j + 2, (2 * j + 1) * 128, p_load_nsa, p_load_fox, c_all, 0, 0, qload, xload, True)
            dma("act", o_y[orow, :], ysb[:], ["ysb"], [f"oy{j}"], "st")

        for q_i in range(nseq):
            def s_load(dst, dname, pool_ap, sc_ap, kt, which, q_i=q_i):
                if kt < 16:
                    if which == "win":
                        dma("sp", dst, win_in[q_i, (kt - 12) * 128:(kt - 11) * 128, :], [], [dname], dname)
                    else:
                        ia = idx[:, q_i * 16 + kt:q_i * 16 + kt + 1]
                        S.op("pool", lambda e: e.indirect_dma_start(out=dst, out_offset=None, in_=pool_ap,
                                                                      in_offset=bass.IndirectOffsetOnAxis(ap=ia, axis=0)),
                             r=["idx"], w=[dname], dma="g_" + dname)
                else:
                    V("pool", "memset", [], [dname], ap=dst, constant=0.0)
                    dma("sp", dst[0:8, :], sc_ap[32 * 128 + 8 * q_i:32 * 128 + 8 * q_i + 8, :], [f"sc{which}32"], [dname], dname)

            for kt in range(16):
                s_load(kvn[:], "kvn", pool_cmp, sc_cmp, kt, "cmp")
                V("dve", "tensor_copy", ["kvn"], ["kv_bf"], out=kv_bf[:], in_=kvn[:])

                def tr3(e):
                    e.transpose(out=pT[:, 0:128], in_=kv_bf[:, 0:128], identity=ident[:])
                    return e.transpose(out=pT[:, 128:256], in_=kv_bf[:, 128:256], identity=ident[:])
                S.op("pe", tr3, r=["kv_bf", "ident"], w=["pT"])
                V("act", "copy", ["pT"], ["kcT"], out=kcT[:, kt * 128:(kt + 1) * 128], in_=pT[:, 0:128])
                V("act", "copy", ["pT"], ["vcT"], out=vcT[:, kt * 128:(kt + 1) * 128], in_=pT[:, 128:256])
            compress(127, 1, False)
            for kt in range(17):
                if kt < 16:
                    ia = idx[:, q_i * 16 + kt:q_i * 16 + kt + 1]
                    S.op("pool", (lambda kt, ia: lambda e: e.indirect_dma_start(out=lfs[:, kt, :], out_offset=None, in_=pool_lf,
                                                                                 in_offset=bass.IndirectOffsetOnAxis(ap=ia, axis=0)))(kt, ia),
                         r=["idx"], w=["lfs"], dma="g_lfs")
                else:
                    V("pool", "memset", [], ["lfs"], ap=lfs[:, 16, :], constant=0.0)
                    dma("sp", lfs[0:8, 16, :], sc_lf[8 * q_i:8 * q_i + 8, :], ["sc_lf"], ["lfs"], "lfs")
            V("pool", "memset", [], ["carry"], ap=carry[:], constant=0.0)
            for kt in range(17):
                def cmm2(e, kt=kt):
                    e.matmul(out=pC[:, 0:8], lhsT=triU[:], rhs=lfs[:, kt, :], start=True, stop=True)
                    return e.matmul(out=pC[:, 8:16], lhsT=ones_f[:], rhs=lfs[:, kt, :], start=True, stop=True)
                S.op("pe", cmm2, r=["lfs", "triU", "ones_f"], w=["pC"])
                V("dve", "tensor_tensor", ["pC", "carry"], ["cT"], out=c_seq[:, kt, :], in0=pC[:, 0:8], in1=carry[:], op=ALU.add)
                V("dve", "tensor_tensor", ["pC", "carry"], ["carry"], out=carry[:], in0=pC[:, 8:16], in1=carry[:], op=ALU.add)

            def s_load_nsa(which, kt, dst, dn, s_load=s_load):
                s_load(dst[:], dn, pool_slc if which == "slc" else None, sc_slc if which == "slc" else sc_win, kt, which)

            def s_load_fox(kt, dst, dn, s_load=s_load):
                s_load(dst[:], dn, pool_fox, sc_fox, kt, "fox")

            def qload_s(q_i=q_i):
                V("pool", "memset", [], PRK + PRQ + ["projq"], ap=proj[:, O_QA:NIN], constant=0.0)
                dma("sp", proj[0:8, O_QA:NIN], qstore[16 * 128 + 8 * q_i:16 * 128 + 8 * q_i + 8, :], ["qst16"], ["projq"], "projq")

            def xload_s(q_i=q_i):
                V("pool", "memset", [], ["xt"], ap=xt[:], constant=0.0)
                dma("sp", xt[0:8, :], x_in[32 * 128 + 8 * q_i:32 * 128 + 8 * q_i + 8, :], [], ["xt"], "xt")
            attn(17, 2048, s_load_nsa, s_load_fox, c_seq, 1, 1, qload_s, xload_s, False)
            dma("act", o_y[16 * 128 + 8 * q_i:16 * 128 + 8 * q_i + 8, :], ysb[0:8, :], ["ysb"], [f"oys{q_i}"], "st")

        print("[build] nsem", S.nsem, {e: len(q) for e, q in S.q.items()}, flush=True)
        with nc.Block() as block:
            S.emit(block)
    return nc


_NC = None
_DEBUG_RETURN_MAPS = False
STOP = 99


def _perm_cols():
    sizes = [512, 128, 128, 128, 128, 128, 128, 24, 512, 512, 512, 512, 8, 512]
    names = ["qa", "kc", "vc", "ks", "vs", "kw", "vw", "ga", "za", "qb", "kb", "vb", "fb", "zb"]
    offs = np.concatenate([[0], np.cumsum(sizes)])
    seg = {n: np.arange(offs[i], offs[i + 1]) for i, n in enumerate(names)}
    order = ["kc", "ks", "kw", "kb", "vc", "vs", "vw", "vb", "fb", "qa", "qb", "za", "zb", "ga"]
    return np.concatenate([seg[n] for n in order])


def kernel(x_prompt, x_sample, cache_nsa_cmp_kv, cache_nsa_slc_kv, cache_nsa_win_kv, cache_fox_kv, cache_fox_logf,
           page_table, g_norm, w_in, b_f, gq_a, gk_cmp, gk_slc, gk_win, pe_cmp_k, pe_cmp_v,
           w_cmp1_k, w_cmp2_k, w_cmp1_v, w_cmp2_v, gq_b, gk_b, w_out):
    global _NC
    f32 = np.float32
    A = lambda v: np.asarray(v, f32)
    x_prompt = A(x_prompt); x_sample = A(x_sample)
    perm = _perm_cols()
    w_p = np.ascontiguousarray(A(w_in)[0][:, perm])
    w_o = np.ascontiguousarray(A(w_out)[0])
    gn = np.ascontiguousarray(A(g_norm)[0].reshape(8, 128).T)
    rep = lambda v: np.ascontiguousarray(np.broadcast_to(A(v).reshape(1, -1), (128, A(v).size)))
    gk = rep(np.concatenate([np.tile(A(gk_slc)[0], 2), np.tile(A(gk_win)[0], 2), np.tile(A(gk_b)[0], 8)]))
    gq = rep(np.concatenate([np.tile(A(gq_a)[0], 8), np.tile(A(gq_b)[0], 8)]))
    gkc = rep(A(gk_cmp)[0])
    bfb = rep(A(b_f)[0])
    w1 = []
    for w_ in (w_cmp1_k, w_cmp1_v):
        a = A(w_)[0].transpose(1, 0, 2).reshape(64, 4096)
        w1.append(np.ascontiguousarray(np.concatenate([a, a], 0)))
    w2 = np.ascontiguousarray(np.concatenate([A(w_cmp2_k)[0], A(w_cmp2_v)[0]], 1))
    peT = np.ascontiguousarray(np.concatenate([A(pe_cmp_k)[0].T, A(pe_cmp_v)[0].T], 1))
    n = np.arange(256); jb = np.arange(64)
    ov = np.clip(np.minimum(n[:, None] * 16 + 32, jb[None, :] * 64 + 64) - np.maximum(n[:, None] * 16, jb[None, :] * 64), 0, None) / 32.0
    msel = np.ascontiguousarray(ov.astype(f32).reshape(2, 128, 64).transpose(1, 0, 2).reshape(128, 128))
    half = 32
    inv = np.power(np.float32(10000.0), -np.arange(half, dtype=f32) / half).astype(f32)
    win = A(cache_nsa_win_kv)[0].reshape(128, 512, 256)
    pool_cmp = A(cache_nsa_cmp_kv)[0].reshape(-1, 256)
    pool_slc = A(cache_nsa_slc_kv)[0].reshape(-1, 256)
    pool_fox = A(cache_fox_kv)[0].reshape(-1, 1024)
    pool_lf = A(cache_fox_logf)[0].reshape(-1, 8)
    pt = np.asarray(page_table, np.int32)
    piota = np.arange(128, dtype=f32).reshape(128, 1)
    in_maps = []
    for c in range(NCORES):
        b, h = c // 2, c % 2
        xb = x_prompt[b].reshape(32, 128, D)
        if h == 1:
            xst = xb
            nat = np.arange(32)
        else:
            xst = np.concatenate([np.zeros((1, 128, D), f32), xb[:31]], 0)
            nat = np.arange(32) - 1
        xsm = x_sample[16 * c:16 * c + 16].reshape(1, 128, D)
        xc = np.ascontiguousarray(np.concatenate([xst, xsm], 0).reshape(NS * 128, D))
        pos = np.zeros((NS, 128), f32)
        for s_ in range(32):
            pos[s_] = nat[s_] * 128 + np.arange(128)
        pos[32] = 2048 + (np.arange(128) % 8)
        ang = pos[:, :, None] * inv[None, None, :]
        cs_ = np.concatenate([np.cos(ang), np.sin(ang)], -1).astype(f32)
        cs_ = np.ascontiguousarray(cs_.transpose(1, 0, 2))
        cval = np.ones((128, 2), f32)
        f0 = np.full((128, 2, 64), -2.0, f32)
        dmin = np.full((128, 2, 64), 1e9, f32)
        tval = np.ones((128, 1), f32)
        if h == 0:
            cval[0:8, 0] = 0.0
            f0[:, 0, 2] = 1e4
            dmin[:, 0, 0:2] = -1.0
            tval[:] = 0.0
        else:
            f0[:, 0, 0] = 1e4
        f0[:, 1, 0] = 1e4
        ptc = np.ascontiguousarray(np.broadcast_to(pt[16 * c:16 * c + 16].reshape(1, 256), (128, 256))).astype(np.int32)
        in_maps.append({"x": xc, "w_in": w_p, "w_out": w_o, "gn": gn, "cs": cs_, "gk": gk, "gq": gq, "gkc": gkc, "bfb": bfb,
                        "win": np.ascontiguousarray(win[16 * c:16 * c + 16]), "w1k": w1[0], "w1v": w1[1], "w2": w2, "peT": peT,
                        "msel": msel, "cval": cval, "f0": np.ascontiguousarray(f0.reshape(128, 128)),
                        "dmin": np.ascontiguousarray(dmin.reshape(128, 128)), "pool_cmp": pool_cmp, "pool_slc": pool_slc,
                        "pool_fox": pool_fox, "pool_lf": pool_lf, "pt": ptc, "piota": piota, "tval": tval})
    if N_SAMPLE_SEQ == 0:
        for m in in_maps:
            for k in ("pool_cmp", "pool_slc", "pool_fox", "pool_lf"):
                m.pop(k)
    if _DEBUG_RETURN_MAPS:
        return in_maps
    if _NC is None:
        _NC = build()
    res = run_bass_kernel_spmd(_NC, in_maps, core_ids=list(range(NCORES)))
    R = res.results
    yp = np.zeros((4, 32, 128, D), f32); ys = np.zeros((128, 8, D), f32)
    cmp_p = np.zeros((4, 32, 128, 256), f32); slc_p = np.zeros((4, 32, 128, 256), f32); win_p = np.zeros((4, 32, 128, 256), f32)
    fox_p = np.zeros((4, 32, 128, 1024), f32); lf_p = np.zeros((4, 32, 128, 8), f32)
    cmp_s = np.zeros((128, 8, 256), f32); slc_s = np.zeros((128, 8, 256), f32); fox_s = np.zeros((128, 8, 1024), f32)
    lf_s = np.zeros((128, 8, 8), f32); win_s = np.zeros((128, 512, 256), f32)
    for c in range(NCORES):
        b, h = c // 2, c % 2
        r = R[c]
        for (dst_p, dst_s, key, wd) in ((yp, ys, "o_y", D), (cmp_p, cmp_s, "o_cmp", 256), (slc_p, slc_s, "o_slc", 256),
                                        (fox_p, fox_s, "o_fox", 1024), (lf_p, lf_s, "o_lf", 8)):
            a = np.asarray(r[key]).reshape(NQ, 128, wd)
            dst_p[b, h::2] = a[:16]
            dst_s[16 * c:16 * c + 16] = a[16].reshape(16, 8, wd)
        win_p[b, h::2] = np.asarray(r["o_win"]).reshape(NQ, 128, 256)[:16]
        win_s[16 * c:16 * c + 16] = np.asarray(r["o_wins"]).reshape(16, 512, 256)
    kv = lambda a, g: a.reshape(a.shape[:-1] + (2, g, 64))
    return (yp.reshape(4, T, D), ys,
            kv(cmp_p.reshape(1, 4, T, 256), 2), kv(cmp_s.reshape(1, 128, 8, 256), 2),
            kv(slc_p.reshape(1, 4, T, 256), 2), kv(slc_s.reshape(1, 128, 8, 256), 2),
            kv(win_p.reshape(1, 4, T, 256)[:, :, T - 512:], 2), kv(win_s.reshape(1, 128, 512, 256), 2),
            kv(fox_p.reshape(1, 4, T, 1024), 8), kv(fox_s.reshape(1, 128, 8, 1024), 8),
            lf_p.reshape(1, 4, T, 8), lf_s.reshape(1, 128, 8, 8))
```

```python
from contextlib import ExitStack
import os
import numpy as np
import ml_dtypes
import concourse.bass as bass
import concourse.mybir as mybir
from concourse.bass_utils import run_bass_kernel_spmd

F32 = mybir.dt.float32
BF16 = mybir.dt.bfloat16
I32 = mybir.dt.int32
ALU = mybir.AluOpType
AF = mybir.ActivationFunctionType
AX = mybir.AxisListType

NCORES = 8
D = 1024
T = 4096
NT = 16
NTILE = NT + 1
EPS = 1e-6
O_KC, O_KS, O_KW, O_KB, O_VC, O_VS, O_VW, O_VB, O_FB = 0, 128, 256, 384, 896, 1024, 1152, 1280, 1792
O_QA, O_QB, O_ZA, O_ZB, O_GA = 1800, 2312, 2824, 3336, 3848
NIN = 3872
CHUNKS = [(0, 512), (512, 512), (1024, 512), (1536, 264), (1800, 512), (2312, 512), (2824, 512), (3336, 512), (3848, 24)]
ENGS = ["sp", "act", "dve", "pool", "pe"]
SEM_LIMIT = 30000


class Sched:
    def __init__(self, nc, stack):
        self.nc = nc
        self.stack = stack
        self.q = {e: [] for e in ENGS}
        self.waited = {e: {} for e in ENGS}
        self.ctr = {}
        self.lastw = {}
        self.readers = {}
        self.nsem = 0
        self.all_tokens = {}
        self.dma_sems = set()

    def _newsem(self, name):
        self.nsem += 1
        return self.stack.enter_context(self.nc.semaphore(f"s{self.nsem}_{name}"))

    def _token(self, key, inc):
        c = self.ctr.get(key)
        if c is None or c[1] + inc > SEM_LIMIT:
            c = [self._newsem(key), 0]
            self.ctr[key] = c
        c[1] += inc
        tok = (c[0], c[1])
        self.all_tokens[id(c[0])] = tok
        return tok

    def op(self, eng, fn, r=(), w=(), dma=None):
        deps = []
        for b in r:
            if b in self.lastw:
                deps.append(self.lastw[b])
        for b in w:
            if b in self.lastw:
                deps.append(self.lastw[b])
            deps.extend(self.readers.get(b, ()))
        waits = {}
        for (s, v) in deps:
            k = id(s)
            if k in self.dma_sems:
                v = self.all_tokens[k][1]
            if self.waited[eng].get(k, 0) >= v:
                continue
            if k not in waits or waits[k][1] < v:
                waits[k] = (s, v)
        for k, (s, v) in waits.items():
            self.waited[eng][k] = v
        if dma is not None:
            tok = self._token("d_" + dma, 16)
            self.dma_sems.add(id(tok[0]))
        else:
            tok = self._token("e_" + eng, 1)
        self.q[eng].append((list(waits.values()), fn, tok, 16 if dma is not None else 1))
        for b in w:
            self.lastw[b] = tok
            self.readers[b] = []
        for b in r:
            if b not in w:
                self.readers.setdefault(b, []).append(tok)
        return tok

    def emit(self, block):
        nc = self.nc
        final = list(self.all_tokens.values())

        def run(engname, e):
            for waits, fn, tok, inc in self.q[engname]:
                for (s, v) in waits:
                    e.wait_ge(s, v)
                fn(e).then_inc(tok[0], inc)
            if engname == "sp":
                for (s, v) in final:
                    e.wait_ge(s, v)

        @block.sync
        def _(e):
            run("sp", e)

        @block.scalar
        def _(e):
            run("act", e)

        @block.vector
        def _(e):
            run("dve", e)

        @block.gpsimd
        def _(e):
            run("pool", e)

        @block.tensor
        def _(e):
            run("pe", e)


NS = 33
NQ = 17
QW = 2072
NPOOL = 2560
N_SAMPLE_SEQ = 16


def build(npq=16, nseq=N_SAMPLE_SEQ):
    nc = bass.Bass("TRN2", target_bir_lowering=False)
    dt = lambda n, s, d=F32, k="ExternalInput": nc.dram_tensor(n, list(s), d, kind=k).ap()
    x_in = dt("x", [NS * 128, D])
    w_in = dt("w_in", [D, NIN])
    w_out_in = dt("w_out", [D, D])
    gn_in = dt("gn", [128, 8])
    cs_in = dt("cs", [128, NS, 64])
    gk_in = dt("gk", [128, 12 * 64])
    gq_in = dt("gq", [128, 16 * 64])
    gkc_in = dt("gkc", [128, 64])
    bf_in = dt("bfb", [128, 8])
    win_in = dt("win", [16, 512, 256])
    w1_in = [dt("w1k", [128, 4096]), dt("w1v", [128, 4096])]
    w2_in = dt("w2", [128, 128])
    pe_in = dt("peT", [64, 64])
    msel_in = dt("msel", [128, 128])
    cval_in = dt("cval", [128, 2])
    f0_in = dt("f0", [128, 128])
    dmin_in = dt("dmin", [128, 128])
    if nseq > 0:
        pool_cmp = dt("pool_cmp", [NPOOL * 128, 256])
        pool_slc = dt("pool_slc", [NPOOL * 128, 256])
        pool_fox = dt("pool_fox", [NPOOL * 128, 1024])
        pool_lf = dt("pool_lf", [NPOOL * 128, 8])
    pt_in = dt("pt", [128, 256], I32)
    pio_in = dt("piota", [128, 1])
    tval_in = dt("tval", [128, 1])
    o_cmp = dt("o_cmp", [NQ * 128, 256], F32, "ExternalOutput")
    o_slc = dt("o_slc", [NQ * 128, 256], F32, "ExternalOutput")
    o_win = dt("o_win", [NQ * 128, 256], F32, "ExternalOutput")
    o_fox = dt("o_fox", [NQ * 128, 1024], F32, "ExternalOutput")
    o_lf = dt("o_lf", [NQ * 128, 8], F32, "ExternalOutput")
    o_wins = dt("o_wins", [16, 512, 256], F32, "ExternalOutput")
    o_y = dt("o_y", [NQ * 128, D], F32, "ExternalOutput")
    sc_cmp = dt("sc_cmp", [NS * 128, 256], F32, "Internal")
    sc_slc = dt("sc_slc", [NS * 128, 256], F32, "Internal")
    sc_win = dt("sc_win", [NS * 128, 256], F32, "Internal")
    sc_fox = dt("sc_fox", [NS * 128, 1024], F32, "Internal")
    sc_lf = dt("sc_lf", [128, 8], F32, "Internal")
    qstore = dt("qstore", [NQ * 128, QW], F32, "Internal")

    with ExitStack() as st:
        sb = lambda n, s, d=F32: st.enter_context(nc.sbuf_tensor("sb_" + n, list(s), d))
        ps = lambda n, s, d=F32: st.enter_context(nc.psum_tensor("ps_" + n, list(s), d))
        S = Sched(nc, st)

        def V(eng, meth, r, w, **kw):
            def fn(e):
                try:
                    return getattr(e, meth)(**kw)
                except Exception:
                    print("FAILED OP", eng, meth, {k: (getattr(v, "shape", v), getattr(v, "ap", None)) for k, v in kw.items()})
                    raise
            return S.op(eng, fn, r=r, w=w)

        def dma(eng, out, in_, r, w, tag):
            return S.op(eng, lambda e: e.dma_start(out=out, in_=in_), r=r, w=w, dma=tag)

        ident = sb("ident", [128, 128], BF16)
        triU = sb("triU", [128, 128])
        ones_f = sb("ones_f", [128, 128])
        sel127 = sb("sel127", [128, 128])
        arena = sb("arena", [128, 8 * NIN], BF16)
        w_bf = arena[:].rearrange("p (c n) -> p c n", c=8)
        WB = [f"w_bf{c}" for c in range(8)]
        wstage = sb("wstage", [128, 4096])
        w1_bf = [sb("w1k_bf", [128, 32, 128], BF16), sb("w1v_bf", [128, 32, 128], BF16)]
        w2_bf = sb("w2_bf", [128, 2, 64], BF16)
        w2_f = sb("w2_f", [128, 128])
        pe_f = sb("pe_f", [64, 64])
        pe_bf = sb("pe_bf", [64, 2, 32], BF16)
        b1 = sb("b1", [128, 2])
        gn = sb("gn_sb", [128, 8])
        cs = sb("cs_sb", [128, NS, 64])
        gk = sb("gk_sb", [128, 12, 64])
        gq = sb("gq_sb", [128, 16, 64])
        gkc = sb("gkc_sb", [128, 64])
        bfb = sb("bfb_sb", [128, 8])
        msel_f = sb("msel_f", [128, 2, 64])
        cval = sb("cval", [128, 2])
        f0 = sb("f0", [128, 2, 64])
        dmin = sb("dmin", [128, 2, 64])
        pt_sb = sb("pt_sb", [128, 256], I32)
        pio = sb("pio", [128, 1])
        tval = sb("tval", [128, 1])
        idx = sb("idx", [128, 256], I32)
        xt = sb("xt", [128, D])
        ss = sb("ss", [128, 1])
        rstd = sb("rstd", [128, 1])
        xs = sb("xs", [128, D], BF16)
        hT = sb("hT", [128, 8, 128], BF16)
        proj = sb("proj", [128, NIN])
        sq = sb("sq", [128, 16 * 64])
        ssh = sb("ssh", [128, 28])
        rsh = sb("rsh", [128, 28])
        rt = [sb(f"rt{i}", [128, 8, 32]) for i in range(4)]
        lf = sb("lf", [128, 8])
        lft = sb("lft", [128, 8])
        carry = sb("carry", [128, 8])
        c_all = sb("c_all", [128, 32, 8])
        c_seq = sb("c_seq", [128, 17, 8])
        kv_bf = sb("kv_bf", [128, 256], BF16)
        kcT = sb("kcT", [128, 4096], BF16)
        vcT = sb("vcT", [128, 4096], BF16)
        hsb = sb("hsb", [128, 256], BF16)
        kcn = sb("kcn", [128, 2, 64])
        kcn_bf = sb("kcn_bf", [128, 128], BF16)
        kcmpT = [sb(f"kcmpT{i}", [128, 256], BF16) for i in range(2)]
        vcaug = [sb(f"vcaug{i}", [128, 2, 2, 65], BF16) for i in range(2)]
        mselv = [sb(f"mselv{i}", [128, 2, 64], BF16) for i in range(2)]
        ones2 = sb("ones2", [128, 2])
        s4 = sb("s4", [128, 4])
        mEC = sb("mEC", [128, 8])
        qa2 = sb("qa2", [128, 512], BF16)
        qaT = sb("qaT", [128, 512], BF16)
        qb_bf = sb("qb_bf", [128, 512], BF16)
        qbT = sb("qbT", [128, 512], BF16)
        kvn = sb("kvn", [128, 256])
        kn_bf = sb("kn_bf", [128, 128], BF16)
        knT = sb("knT", [128, 128], BF16)
        vaugn = sb("vaugn", [128, 2, 65], BF16)
        kb_bf = sb("kb_bf", [128, 512], BF16)
        kbT = sb("kbT", [128, 512], BF16)
        vaugf = sb("vaugf", [128, 8, 65], BF16)
        PT = sb("PT", [128, 512], BF16)
        den = sb("den", [128, 8])
        obr = sb("obr", [128, 8, 64])
        gsig = sb("gsig", [128, 24])
        unsa = sb("unsa", [128, 8, 64])
        imp = sb("imp", [128, 64])
        impt = sb("impt", [128, 8, 64])
        imw = sb("imw", [128, 64])
        mx8 = sb("mx8", [128, 8])
        selb = sb("selb", [128, 2, 64], BF16)
        mk_tm = sb("mk_tm", [128, 128], BF16)
        mk = sb("mk", [128, 128], BF16)
        cref = sb("cref", [128, 8])
        bias = sb("bias", [128, 33, 8])
        lfs = sb("lfs", [128, 17, 8])
        zs = sb("zs", [128, 1024])
        aoff = [0]

        def carve(ncols_bf16):
            a = arena[:, aoff[0]:aoff[0] + ncols_bf16]
            aoff[0] += ncols_bf16
            return a
        w_out_bf = carve(8 * 1024).rearrange("p (c n) -> p c n", c=8)
        u_bf = carve(1024)
        uT = carve(1024)
        kvf = wstage[:, 0:1024]
        u = wstage[:, 1024:2048]
        ysb = wstage[:, 2048:3072]
        kvf2 = wstage[:, 3072:4096]
        PT2 = carve(512)
        kn_bf2 = carve(128)
        knT2 = carve(128)
        vaugn2 = carve(130).rearrange("p (a b) -> p a b", b=65)
        kb_bf2 = carve(512)
        kbT2 = carve(512)
        vaugf2 = carve(520).rearrange("p (a b) -> p a b", b=65)
        mk_tm2 = carve(128)
        mk2 = carve(128)
        ARENA_BUFS = ["w_out_bf", "u_bf", "uT", "PT1", "kn_bf1", "knT1", "vaugn1", "kb_bf1", "kbT1", "vaugf1", "mk_tm1", "mk1", "kvn1"]

        pT = ps("pT", [128, 1024], BF16)
        pP = [ps(f"pP{i}", [128, 512]) for i in range(2)]
        pS = ps("pS", [128, 512])
        acc = ps("acc", [128, 2, 512])
        pI = ps("pI", [128, 512])
        pC = ps("pC", [128, 16])

        V("pool", "memset", [], ["ident"], ap=ident[:], constant=0.0)
        V("pool", "affine_select", ["ident"], ["ident"], out=ident[:], in_=ident[:], pattern=[[-1, 128]],
          compare_op=ALU.not_equal, fill=1.0, base=0, channel_multiplier=1)
        V("pool", "memset", [], ["ones_f"], ap=ones_f[:], constant=1.0)
        V("pool", "memset", [], ["triU"], ap=triU[:], constant=1.0)
        V("pool", "affine_select", ["triU"], ["triU"], out=triU[:], in_=triU[:], pattern=[[1, 128]],
          compare_op=ALU.is_ge, fill=0.0, base=0, channel_multiplier=-1)
        V("pool", "memset", [], ["sel127"], ap=sel127[:], constant=1.0)
        V("pool", "affine_select", ["sel127"], ["sel127"], out=sel127[:], in_=sel127[:], pattern=[[0, 128]],
          compare_op=ALU.is_ge, fill=0.0, base=-127, channel_multiplier=1)
        V("pool", "memset", [], ["ones2"], ap=ones2[:], constant=1.0)
        V("pool", "memset", [], ["mEC"], ap=mEC[:], constant=1.0)
        V("pool", "affine_select", ["mEC"], ["mEC"], out=mEC[:, 0:2], in_=mEC[:, 0:2], pattern=[[-64, 2]], compare_op=ALU.is_ge, fill=0.0,
          base=-64, channel_multiplier=1)
        V("pool", "affine_select", ["mEC"], ["mEC"], out=mEC[:, 4:6], in_=mEC[:, 4:6], pattern=[[-64, 2]], compare_op=ALU.is_ge, fill=0.0,
          base=0, channel_multiplier=1)
        V("dve", "tensor_scalar", ["mEC"], ["mEC"], out=mEC[:, 2:4], in0=mEC[:, 0:2], scalar1=-1e4, scalar2=1e4, op0=ALU.mult, op1=ALU.add)
        V("dve", "tensor_scalar", ["mEC"], ["mEC"], out=mEC[:, 6:8], in0=mEC[:, 4:6], scalar1=-1.0, scalar2=None, op0=ALU.add)
        V("pool", "memset", [], ["carry"], ap=carry[:], constant=0.0)
        V("pool", "memset", [], ["zs"], ap=zs[:], constant=0.0)
        for (t_, i_, nm) in ((gn, gn_in, "gn"), (cs, cs_in, "cs"), (gk, gk_in, "gk"), (gq, gq_in, "gq"), (gkc, gkc_in, "gkc"),
                             (bfb, bf_in, "bfb"), (msel_f, msel_in, "msel_f"), (cval, cval_in, "cval"), (f0, f0_in, "f0"),
                             (dmin, dmin_in, "dmin"), (pt_sb, pt_in, "pt_sb"), (pio, pio_in, "pio"), (w2_f, w2_in, "w2_f"),
                             (pe_f, pe_in, "pe_f"), (tval, tval_in, "tval")):
            src = t_[:]
            if len(t_.shape) == 3:
                src = t_[:].rearrange("p a b -> p (a b)")
            dma("sp", src, i_, [], [nm], nm)
        V("dve", "tensor_copy", ["w2_f"], ["w2_bf"], out=w2_bf[:].rearrange("p a b -> p (a b)"), in_=w2_f[:])
        V("dve", "tensor_copy", ["pe_f"], ["pe_bf"], out=pe_bf[:].rearrange("p a b -> p (a b)"), in_=pe_f[:])
        V("dve", "tensor_scalar", ["pt_sb", "pio"], ["idx"], out=idx[:], in0=pt_sb[:], scalar1=128.0, scalar2=pio[:, 0:1],
          op0=ALU.mult, op1=ALU.add)
        for kv in range(2):
            dma("sp", wstage[:], w1_in[kv], [], ["wstage"], "wstage")
            V("dve", "tensor_copy", ["wstage"], [f"w1bf{kv}"], out=w1_bf[kv][:].rearrange("p a b -> p (a b)"), in_=wstage[:])

            def b1mm(e, kv=kv):
                ins = None
                for l in range(32):
                    ins = e.matmul(out=pC[:, kv:kv + 1], lhsT=w1_bf[kv][0:64, l, :], rhs=pe_bf[:, kv, l:l + 1], start=(l == 0), stop=(l == 31))
                return ins
            S.op("pe", b1mm, r=[f"w1bf{kv}", "pe_bf"], w=["pC"])
            V("dve", "tensor_copy", ["pC"], ["b1"], out=b1[:, kv:kv + 1], in_=pC[:, kv:kv + 1])
        for c in range(8):
            dma("sp", wstage[:, 0:NIN], w_in[c * 128:(c + 1) * 128, :], [], ["wstage"], "wstage")
            V("dve" if c % 2 == 0 else "pool", "tensor_scalar", ["wstage", "gn"], [WB[c]], out=w_bf[:, c, :], in0=wstage[:, 0:NIN],
              scalar1=gn[:, c:c + 1], scalar2=None, op0=ALU.mult)
        for q_i in range(16):
            dma("act", o_wins[q_i, 0:504, :], win_in[q_i, 8:512, :], [], [f"owins{q_i}"], f"owins{q_i % 4}")

        def transposes(in_aps, r, w_out_ap, wname, eng="act"):
            n = len(in_aps)

            def tr(e):
                ins = None
                for k, a in enumerate(in_aps):
                    ins = e.transpose(out=pT[:, k * 128:(k + 1) * 128], in_=a, identity=ident[:])
                return ins
            S.op("pe", tr, r=list(r) + ["ident"], w=["pT"])
            if eng == "act":
                V("act", "copy", ["pT"], [wname], out=w_out_ap, in_=pT[:, 0:n * 128])
            else:
                V("dve", "tensor_copy", ["pT"], [wname], out=w_out_ap, in_=pT[:, 0:n * 128])

        def head_norm(reg, nh, hoff, gt, gname, R, W):
            V("dve", "tensor_tensor", R, ["sq"], out=sq[:, 0:nh * 64], in0=reg, in1=reg, op=ALU.mult)
            V("dve", "tensor_reduce", ["sq"], ["ssh"], out=ssh[:, hoff:hoff + nh], in_=sq[:, 0:nh * 64].rearrange("p (h d) -> p h d", d=64),
              axis=AX.X, op=ALU.add)
            V("dve", "tensor_scalar", ["ssh"], ["rsh"], out=rsh[:, hoff:hoff + nh], in0=ssh[:, hoff:hoff + nh], scalar1=1.0 / 64,
              scalar2=EPS, op0=ALU.mult, op1=ALU.add)
            V("act", "activation", ["rsh"], ["rsh"], out=rsh[:, hoff:hoff + nh], in_=rsh[:, hoff:hoff + nh], func=AF.Sqrt)
            V("dve", "reciprocal", ["rsh"], ["rsh"], out=rsh[:, hoff:hoff + nh], in_=rsh[:, hoff:hoff + nh])
            r3 = reg.rearrange("p (h d) -> p h d", d=64)
            V("dve", "tensor_tensor", R + ["rsh"], W, out=r3, in0=r3,
              in1=rsh[:, hoff:hoff + nh].unsqueeze(2).to_broadcast([128, nh, 64]), op=ALU.mult)
            V("pool", "tensor_tensor", R + [gname], W, out=r3, in0=r3, in1=gt, op=ALU.mult)

        PRK = [f"projc{ci}" for ci in range(4)]
        PRQ = [f"projc{ci}" for ci in range(4, 9)]
        for t in range(NS):
            own = (t % 2 == 1) or (t == 32)
            oj = 16 if t == 32 else (t - 1) // 2
            rows = slice(t * 128, (t + 1) * 128)
            dma("sp", xt[:], x_in[rows, :], [], ["xt"], "xt")
            V("act", "activation", ["xt"], ["xs", "ss"], out=xs[:], in_=xt[:], func=AF.Square, accum_out=ss[:])
            V("dve", "tensor_scalar", ["ss"], ["rstd"], out=rstd[:], in0=ss[:], scalar1=1.0 / D, scalar2=EPS, op0=ALU.mult, op1=ALU.add)
            V("act", "activation", ["rstd"], ["rstd"], out=rstd[:], in_=rstd[:], func=AF.Sqrt)
            V("dve", "reciprocal", ["rstd"], ["rstd"], out=rstd[:], in_=rstd[:])
            V("dve", "tensor_scalar", ["xt", "rstd"], ["xs"], out=xs[:], in0=xt[:], scalar1=rstd[:, 0:1], scalar2=None, op0=ALU.mult)
            transposes([xs[:, c * 128:(c + 1) * 128] for c in range(8)], ["xs"], hT[:].rearrange("p a b -> p (a b)"), "hT")
            nch = 9 if own else 4
            for ci in range(nch):
                off, wd = CHUNKS[ci]
                pb = ci % 2

                def mm(e, off=off, wd=wd, pb=pb):
                    ins = None
                    for c in range(8):
                        ins = e.matmul(out=pP[pb][:, 0:wd], lhsT=hT[:, c, :], rhs=w_bf[:, c, off:off + wd], start=(c == 0), stop=(c == 7))
                    return ins
                S.op("pe", mm, r=["hT"] + WB, w=[f"pP{pb}"])
                if ci % 2 == 0:
                    V("act", "copy", [f"pP{pb}"], [f"projc{ci}"], out=proj[:, off:off + wd], in_=pP[pb][:, 0:wd])
                else:
                    V("dve", "tensor_copy", [f"pP{pb}"], [f"projc{ci}"], out=proj[:, off:off + wd], in_=pP[pb][:, 0:wd])
            head_norm(proj[:, O_KS:O_KS + 768], 12, 0, gk[:], "gk", PRK, PRK)
            if own:
                head_norm(proj[:, O_QA:O_QA + 1024], 16, 12, gq[:], "gq", PRQ, PRQ)
            for (off, nh, eng, PR) in ((O_KC, 6, "dve", PRK), (O_QA, 8, "pool", PRQ)):
                if off == O_QA and not own:
                    continue
                r4 = proj[:, off:off + nh * 64].rearrange("p (h two d) -> p h two d", two=2, d=32)
                x1, x2 = r4[:, :, 0, :], r4[:, :, 1, :]
                cosb = cs[:, t, 0:32].unsqueeze(1).to_broadcast([128, nh, 32])
                sinb = cs[:, t, 32:64].unsqueeze(1).to_broadcast([128, nh, 32])
                k = 0 if eng == "dve" else 2
                ta, tb = rt[k][:, 0:nh, :], rt[k + 1][:, 0:nh, :]
                RT = [f"rt{k}", f"rt{k + 1}"]
                V(eng, "tensor_tensor", PR + ["cs"], [RT[0]], out=ta, in0=x1, in1=sinb, op=ALU.mult)
                V(eng, "tensor_tensor", PR + ["cs"], [RT[1]], out=tb, in0=x2, in1=sinb, op=ALU.mult)
                V(eng, "tensor_tensor", PR + ["cs"], PR, out=x1, in0=x1, in1=cosb, op=ALU.mult)
                V(eng, "tensor_tensor", PR + ["cs"], PR, out=x2, in0=x2, in1=cosb, op=ALU.mult)
                V(eng, "tensor_tensor", PR + RT, PR, out=x1, in0=x1, in1=tb, op=ALU.subtract)
                V(eng, "tensor_tensor", PR + RT, PR, out=x2, in0=x2, in1=ta, op=ALU.add)
            V("dve", "tensor_tensor", PRK + ["bfb"], ["lft"], out=lft[:], in0=proj[:, O_FB:O_FB + 8], in1=bfb[:], op=ALU.add)
            V("act", "activation", ["lft"], ["lft"], out=lft[:], in_=lft[:], func=AF.Exp, scale=-1.0)
            V("act", "activation", ["lft"], ["lft"], out=lft[:], in_=lft[:], func=AF.Ln, bias=1.0)
            V("dve", "tensor_scalar", ["lft"], ["lf"], out=lf[:], in0=lft[:], scalar1=-1.0, scalar2=None, op0=ALU.mult)
            orow = slice(oj * 128, (oj + 1) * 128)
            for (oap, sap, ka, va, wdt, nm) in ((o_cmp, sc_cmp, O_KC, O_VC, 128, "cmp"), (o_slc, sc_slc, O_KS, O_VS, 128, "slc"),
                                                (o_win, sc_win, O_KW, O_VW, 128, "win"), (o_fox, sc_fox, O_KB, O_VB, 512, "fox")):
                dma("act", sap[rows, 0:wdt], proj[:, ka:ka + wdt], PRK, [f"sc{nm}{t}"], f"stsc{t % 8}")
                dma("act", sap[rows, wdt:2 * wdt], proj[:, va:va + wdt], PRK, [f"sc{nm}{t}"], f"stsc{t % 8}")
                if own:
                    dma("act", oap[orow, 0:wdt], proj[:, ka:ka + wdt], PRK, [f"o{nm}k{t}"], "st")
                    dma("act", oap[orow, wdt:2 * wdt], proj[:, va:va + wdt], PRK, [f"o{nm}v{t}"], "st")
            if own:
                dma("act", o_lf[orow, :], lf[:], ["lf"], [f"olf{t}"], "st")
                dma("act", qstore[orow, :], proj[:, O_QA:NIN], PRQ, [f"qst{oj}"], f"stq{oj % 4}")
            if t == 32:
                dma("act", sc_lf, lf[:], ["lf"], ["sc_lf"], "stsc")
                for q_i in range(16):
                    dma("act", o_wins[q_i, 504:512, 0:128], proj[q_i * 8:(q_i + 1) * 8, O_KW:O_KW + 128], PRK + [f"owins{q_i}"],
                        [f"owinsn{q_i}"], "st")
                    dma("act", o_wins[q_i, 504:512, 128:256], proj[q_i * 8:(q_i + 1) * 8, O_VW:O_VW + 128], PRK + [f"owins{q_i}"],
                        [f"owinsm{q_i}"], "st")
            else:
                def cmm(e):
                    e.matmul(out=pC[:, 0:8], lhsT=triU[:], rhs=lf[:], start=True, stop=True)
                    return e.matmul(out=pC[:, 8:16], lhsT=ones_f[:], rhs=lf[:], start=True, stop=True)
                S.op("pe", cmm, r=["lf", "triU", "ones_f"], w=["pC"])
                V("dve", "tensor_tensor", ["pC", "carry"], [f"c_all{t}"], out=c_all[:, t, :], in0=pC[:, 0:8], in1=carry[:], op=ALU.add)
                V("dve", "tensor_tensor", ["pC", "carry"], ["carry"], out=carry[:], in0=pC[:, 8:16], in1=carry[:], op=ALU.add)
                V("dve", "tensor_copy", PRK, ["kv_bf"], out=kv_bf[:, 0:128], in_=proj[:, O_KC:O_KC + 128])
                V("dve", "tensor_copy", PRK, ["kv_bf"], out=kv_bf[:, 128:256], in_=proj[:, O_VC:O_VC + 128])

                def tr2(e):
                    e.transpose(out=pT[:, 0:128], in_=kv_bf[:, 0:128], identity=ident[:])
                    return e.transpose(out=pT[:, 128:256], in_=kv_bf[:, 128:256], identity=ident[:])
                S.op("pe", tr2, r=["kv_bf", "ident"], w=["pT"])
                V("act", "copy", ["pT"], ["kcT"], out=kcT[:, t * 128:(t + 1) * 128], in_=pT[:, 0:128])
                V("act", "copy", ["pT"], ["vcT"], out=vcT[:, t * 128:(t + 1) * 128], in_=pT[:, 128:256])

        AW = ARENA_BUFS + WB
        dma("sp", wstage[:, 0:1024], w_out_in[0:128, :], [], ["wstage"], "wstage")
        for c in range(8):
            V("dve", "tensor_copy", ["wstage"], AW if c == 0 else ["w_out_bf"], out=w_out_bf[:, c, :], in_=wstage[:, 0:1024])
            if c < 7:
                dma("sp", wstage[:, 0:1024], w_out_in[(c + 1) * 128:(c + 2) * 128, :], [], ["wstage"], "wstage")

        def compress(ncmp, slot, use_cval):
            nt_n = (ncmp + 127) // 128
            V("pool", "memset", [], [f"vcaug{slot}"], ap=vcaug[slot][:].rearrange("p a b c -> p (a b c)"), constant=0.0)
            V("pool", "memset", [], [f"kcmpT{slot}"], ap=kcmpT[slot][:], constant=0.0)
            for kv in range(2):
                src, sname = (kcT, "kcT") if kv == 0 else (vcT, "vcT")
                for g in range(2):
                    def hmm(e, kv=kv, g=g, src=src):
                        ins = None
                        for l in range(32):
                            ins = e.matmul(out=pP[0][:, 0:ncmp], lhsT=w1_bf[kv][64 * g:64 * g + 64, l, :],
                                           rhs=src[64 * g:64 * g + 64, l:l + 16 * (ncmp - 1) + 1:16], start=(l == 0), stop=(l == 31))
                        return ins
                    S.op("pe", hmm, r=[sname, f"w1bf{kv}"], w=["pP0"])
                    V("act", "activation", ["pP0", "b1"], ["hsb"], out=hsb[:, 0:ncmp], in_=pP[0][:, 0:ncmp], func=AF.Silu, bias=b1[:, kv:kv + 1])
                    for nt in range(nt_n):
                        n0 = nt * 128
                        nn = min(128, ncmp - n0)
                        S.op("pe", (lambda nt, n0, nn, kv: lambda e: e.matmul(out=pP[1][0:nn, nt * 64:nt * 64 + 64], lhsT=hsb[:, n0:n0 + nn],
                                                                               rhs=w2_bf[:, kv, :], start=True, stop=True))(nt, n0, nn, kv),
                             r=["hsb", "w2_bf"], w=["pP1"])
                        if kv == 0:
                            V("dve", "tensor_copy", ["pP1"], ["kcn"], out=kcn[0:nn, nt, :], in_=pP[1][0:nn, nt * 64:nt * 64 + 64])
                        else:
                            cv = cval[0:nn, nt:nt + 1] if use_cval else ones2[0:nn, 0:1]
                            V("dve", "tensor_scalar", ["pP1", "cval", "ones2"], [f"vcaug{slot}"], out=vcaug[slot][0:nn, nt, g, 0:64],
                              in0=pP[1][0:nn, nt * 64:nt * 64 + 64], scalar1=cv, scalar2=None, op0=ALU.mult)
                            V("dve", "tensor_copy", ["cval", "ones2"], [f"vcaug{slot}"], out=vcaug[slot][0:nn, nt, g, 64:65], in_=cv)
                    if kv == 0:
                        for nt in range(nt_n):
                            nn = min(128, ncmp - nt * 128)
                            V("dve", "tensor_tensor", ["kcn"], ["sq"], out=sq[0:nn, 0:64], in0=kcn[0:nn, nt, :], in1=kcn[0:nn, nt, :], op=ALU.mult)
                            V("dve", "tensor_reduce", ["sq"], ["s4"], out=s4[0:nn, 0:1], in_=sq[0:nn, 0:64], axis=AX.X, op=ALU.add)
                            V("dve", "tensor_scalar", ["s4"], ["s4"], out=s4[0:nn, 1:2], in0=s4[0:nn, 0:1], scalar1=1.0 / 64, scalar2=EPS,
                              op0=ALU.mult, op1=ALU.add)
                            V("act", "activation", ["s4"], ["s4"], out=s4[0:nn, 2:3], in_=s4[0:nn, 1:2], func=AF.Sqrt)
                            V("dve", "reciprocal", ["s4"], ["s4"], out=s4[0:nn, 3:4], in_=s4[0:nn, 2:3])
                            V("dve", "tensor_scalar", ["kcn", "s4"], ["kcn"], out=kcn[0:nn, nt, :], in0=kcn[0:nn, nt, :], scalar1=s4[0:nn, 3:4],
                              scalar2=None, op0=ALU.mult)
                            V("pool", "memset", [], ["kcn_bf"], ap=kcn_bf[:, 64 * g:64 * g + 64], constant=0.0)
                            V("pool", "tensor_tensor", ["kcn", "gkc"], ["kcn_bf"], out=kcn_bf[0:nn, 64 * g:64 * g + 64], in0=kcn[0:nn, nt, :],
                              in1=gkc[0:nn, :], op=ALU.mult)
                            S.op("pe", lambda e: e.transpose(out=pT[:, 0:128], in_=kcn_bf[:], identity=ident[:]), r=["kcn_bf", "ident"], w=["pT"])
                            V("act", "copy", ["pT"], [f"kcmpT{slot}"], out=kcmpT[slot][64 * g:64 * g + 64, nt * 128:(nt + 1) * 128],
                              in_=pT[64 * g:64 * g + 64, 0:128])
            for nt in range(2):
                cv = cval[:, nt:nt + 1] if use_cval else ones2[:, 0:1]
                V("dve", "tensor_scalar", ["msel_f", "cval", "ones2"], [f"mselv{slot}"], out=mselv[slot][:, nt, :], in0=msel_f[:, nt, :],
                  scalar1=cv, scalar2=None, op0=ALU.mult)

        def sel_mask(eng_r, buf, ap, pattern, base, cm, fill=0.0):
            V("pool", "affine_select", eng_r + [buf], [buf], out=ap, in_=ap, pattern=pattern, compare_op=ALU.is_ge, fill=fill,
              base=base, channel_multiplier=cm)

        def norm_evac(dst3, nheads_total=8):
            for g in range(2):
                a3 = acc[:, g, 0:260].rearrange("p (r e) -> p r e", e=65)
                V("dve", "tensor_scalar", ["acc"], ["den"], out=den[:, 4 * g:4 * g + 4], in0=a3[:, :, 64], scalar1=1e-30, scalar2=None, op0=ALU.max)
                V("dve", "reciprocal", ["den"], ["den"], out=den[:, 4 * g:4 * g + 4], in_=den[:, 4 * g:4 * g + 4])
                V("dve", "tensor_tensor", ["acc", "den"], ["obr"], out=dst3[:, 4 * g:4 * g + 4, :], in0=a3[:, :, 0:64],
                  in1=den[:, 4 * g:4 * g + 4].unsqueeze(2).to_broadcast([128, 4, 64]), op=ALU.mult)

        kvn2 = carve(512).bitcast(F32)
        B_PT = [(PT, "PT"), (PT2, "PT1")]
        B_PS = [(pS, "pS"), (pP[1], "pP1")]
        B_KVN = [(kvn, "kvn"), (kvn2, "kvn1")]
        B_KNB = [(kn_bf, "kn_bf"), (kn_bf2, "kn_bf1")]
        B_KNT = [(knT, "knT"), (knT2, "knT1")]
        B_VAN = [(vaugn, "vaugn"), (vaugn2, "vaugn1")]
        B_KVF = [(kvf, "kvf"), (kvf2, "kvf1")]
        B_KBB = [(kb_bf, "kb_bf"), (kb_bf2, "kb_bf1")]
        B_KBT = [(kbT, "kbT"), (kbT2, "kbT1")]
        B_VAF = [(vaugf, "vaugf"), (vaugf2, "vaugf1")]
        B_MKT = [(mk_tm, "mk_tm"), (mk_tm2, "mk_tm1")]
        B_MK = [(mk, "mk"), (mk2, "mk1")]

        def ones_col(ap, name, kt, tv):
            V("pool", "memset", [], [name], ap=ap, constant=1.0)
            if kt == 0 and tv:
                V("dve", "tensor_scalar", ["tval", name], [name], out=ap, in0=ap, scalar1=tval[:, 0:1], scalar2=None, op0=ALU.mult)

        def attn(nk, s0, load_nsa, load_fox, cT, cslot, fslot, qload, xload, tv):
            kd = nk - 1
            PRQ_ = ["projq"]
            qload()
            R_ = PRQ_
            V("act", "activation", R_, ["gsig"], out=gsig[:], in_=proj[:, O_GA:O_GA + 24], func=AF.Exp, scale=-1.0)
            V("dve", "tensor_scalar", ["gsig"], ["gsig"], out=gsig[:], in0=gsig[:], scalar1=1.0, scalar2=None, op0=ALU.add)
            V("dve", "reciprocal", ["gsig"], ["gsig"], out=gsig[:], in_=gsig[:])
            V("act", "activation", R_, ["zs"], out=zs[:], in_=proj[:, O_ZA:O_ZA + 1024], func=AF.Silu)
            V("dve", "tensor_copy", R_, ["qa2"], out=qa2[:].rearrange("p (r g d) -> p r g d", r=4, g=2),
              in_=proj[:, O_QA:O_QA + 512].rearrange("p (g r d) -> p r g d", g=2, r=4))
            transposes([qa2[:, r * 128:(r + 1) * 128] for r in range(4)], ["qa2"], qaT[:], "qaT")
            V("dve", "tensor_copy", R_, ["qb_bf"], out=qb_bf[:], in_=proj[:, O_QB:O_QB + 512])
            transposes([qb_bf[:, r * 128:(r + 1) * 128] for r in range(4)], ["qb_bf"], qbT[:], "qbT")

            if STOP <= 0:
                return
            for g in range(2):
                for nt in range(2):
                    (PT_, PTn), (pS_, pSn) = B_PT[nt], B_PS[nt]
                    S.op("pe", (lambda g, nt, pS_: lambda e: e.matmul(out=pS_[:], lhsT=kcmpT[cslot][64 * g:64 * g + 64, nt * 128:(nt + 1) * 128],
                                                                 rhs=qaT[64 * g:64 * g + 64, :], start=True, stop=True))(g, nt, pS_),
                         r=[f"kcmpT{cslot}", "qaT"], w=[pSn])
                    V("act", "activation", [pSn], [PTn], out=PT_[:], in_=pS_[:], func=AF.Exp, scale=0.125)
                    vis_all = (s0 - 31 - 2048 * nt - 16 * 127) >= 0
                    vis_none = (s0 + 127 - 31 - 2048 * nt) < 0
                    if vis_none:
                        V("pool", "memset", [], [PTn], ap=PT_[:], constant=0.0)
                    elif not vis_all:
                        sel_mask([], PTn, PT_[:].rearrange("p (r q) -> p r q", r=4), [[0, 4], [1, 128]], s0 - 31 - 2048 * nt, -16)

                    def pv(e, g=g, nt=nt, PT=PT_):
                        ins = None
                        for r in range(4):
                            e.matmul(out=acc[:, g, r * 65:(r + 1) * 65], lhsT=PT[:, r * 128:(r + 1) * 128], rhs=vcaug[cslot][:, nt, g, :],
                                     start=(nt == 0 and r == 0), stop=(nt == 1 and r == 3))
                            ins = e.matmul(out=pI[:, (4 * g + r) * 64:(4 * g + r + 1) * 64], lhsT=PT[:, r * 128:(r + 1) * 128],
                                           rhs=mselv[cslot][:, nt, :], start=(g == 0 and nt == 0 and r == 0), stop=(g == 1 and nt == 1 and r == 3))
                        return ins
                    S.op("pe", pv, r=[PTn, f"vcaug{cslot}", f"mselv{cslot}"], w=["acc", "pI"])
            norm_evac(obr)
            V("dve", "tensor_tensor", ["obr", "gsig"], ["unsa"], out=unsa[:], in0=obr[:],
              in1=gsig[:].rearrange("p (h c) -> p h c", c=3)[:, :, 0:1].to_broadcast([128, 8, 64]), op=ALU.mult)
            if STOP <= 1:
                return
            V("dve", "tensor_tensor", ["pI", "den"], ["impt"], out=impt[:], in0=pI[:].rearrange("p (h j) -> p h j", j=64),
              in1=den[:].unsqueeze(2).to_broadcast([128, 8, 64]), op=ALU.mult)
            for g in range(2):
                V("dve", "tensor_tensor", ["impt"], ["imp"], out=imp[:], in0=impt[:, 4 * g, :], in1=impt[:, 4 * g + 1, :], op=ALU.add)
                V("dve", "tensor_tensor", ["impt", "imp"], ["imp"], out=imp[:], in0=imp[:], in1=impt[:, 4 * g + 2, :], op=ALU.add)
                V("dve", "tensor_tensor", ["impt", "imp"], ["imp"], out=imp[:], in0=imp[:], in1=impt[:, 4 * g + 3, :], op=ALU.add)
                V("dve", "tensor_tensor", ["imp", "dmin"], ["imp"], out=imp[:], in0=imp[:], in1=dmin[:, fslot, :], op=ALU.min)
                b0 = s0 // 64
                xx = imp[:, b0:b0 + 2]
                V("dve", "tensor_tensor", ["imp", "mEC"], ["imp"], out=xx, in0=xx, in1=mEC[:, 0:2], op=ALU.mult)
                V("dve", "tensor_tensor", ["imp", "mEC"], ["imp"], out=xx, in0=xx, in1=mEC[:, 2:4], op=ALU.add)
                V("dve", "tensor_tensor", ["imp", "mEC"], ["imp"], out=xx, in0=xx, in1=mEC[:, 4:6], op=ALU.mult)
                V("dve", "tensor_tensor", ["imp", "mEC"], ["imp"], out=xx, in0=xx, in1=mEC[:, 6:8], op=ALU.add)
                if b0 + 2 < 64:
                    V("pool", "memset", ["imp"], ["imp"], ap=imp[:, b0 + 2:64], constant=-1.0)
                V("dve", "tensor_tensor", ["imp", "f0"], ["imp"], out=imp[:], in0=imp[:], in1=f0[:, fslot, :], op=ALU.max)
                V("dve", "max", ["imp"], ["mx8"], out=mx8[:], in_=imp[:])
                V("dve", "match_replace", ["imp", "mx8"], ["imw"], out=imw[:], in_to_replace=mx8[:], in_values=imp[:], imm_value=-3.0)
                V("dve", "max", ["imw"], ["mx8"], out=mx8[:], in_=imw[:])
                V("dve", "tensor_scalar", ["imp", "mx8"], ["imw"], out=imw[:], in0=imp[:], scalar1=mx8[:, 7:8], scalar2=None, op0=ALU.is_ge)
                V("dve", "tensor_copy", ["imw"], ["selb"], out=selb[:, 0, :], in_=imw[:])
                if STOP <= 2:
                    continue
                def selA(kt, g=g):
                    pp = kt % 2
                    (PT_, PTn), (pS_, pSn), (kvn_, kvnn), (knb_, knbn), (knT_, knTn), (van_, vann), (mkt_, mktn), (mk_, mkn) = (
                        B_PT[pp], B_PS[pp], B_KVN[pp], B_KNB[pp], B_KNT[pp], B_VAN[pp], B_MKT[pp], B_MK[pp])
                    load_nsa("slc", kt, kvn_, kvnn)
                    V("dve", "tensor_copy", [kvnn], [knbn], out=knb_[:], in_=kvn_[:, 0:128])
                    transposes([knb_[:]], [knbn], knT_[:], knTn)
                    ones_col(van_[:, :, 64:65], vann, kt, tv)
                    V("pool", "tensor_copy", [kvnn], [vann], out=van_[:, :, 0:64], in_=kvn_[:, 128:256].rearrange("p (g d) -> p g d", g=2))
                    V("dve", "tensor_copy", ["selb"], [mktn], out=mkt_[:].rearrange("p (a b) -> p a b", a=2),
                      in_=selb[:, 0, 2 * kt:2 * kt + 2].unsqueeze(2).to_broadcast([128, 2, 64]))
                    transposes([mkt_[:]], [mktn], mk_[:], mkn)
                    S.op("pe", (lambda g, pS_, knT_: lambda e: e.matmul(out=pS_[:], lhsT=knT_[64 * g:64 * g + 64, :], rhs=qaT[64 * g:64 * g + 64, :],
                                                                        start=True, stop=True))(g, pS_, knT_), r=[knTn, "qaT"], w=[pSn])
                    V("act", "activation", [pSn], [PTn], out=PT_[:], in_=pS_[:], func=AF.Exp, scale=0.125)

                def selB(kt, g=g):
                    pp = kt % 2
                    (PT_, PTn), (van_, vann), (mk_, mkn) = B_PT[pp], B_VAN[pp], B_MK[pp]
                    V("dve", "tensor_tensor", [PTn, mkn], [PTn], out=PT_[:].rearrange("p (r q) -> p r q", r=4),
                      in0=PT_[:].rearrange("p (r q) -> p r q", r=4), in1=mk_[:].unsqueeze(1).to_broadcast([128, 4, 128]), op=ALU.mult)
                    if kt == kd:
                        sel_mask([], PTn, PT_[:].rearrange("p (r q) -> p r q", r=4), [[0, 4], [1, 128]], 0, -1)

                    def pv2(e, g=g, kt=kt, PT=PT_, vaugn=van_):
                        ins = None
                        for r in range(4):
                            ins = e.matmul(out=acc[:, g, r * 65:(r + 1) * 65], lhsT=PT[:, r * 128:(r + 1) * 128], rhs=vaugn[:, g, :],
                                           start=(kt == 0 and r == 0), stop=(kt == nk - 1 and r == 3))
                        return ins
                    S.op("pe", pv2, r=[PTn, vann], w=["acc"])
                for i_ in range(nk + 1):
                    if i_ < nk:
                        selA(i_)
                    if i_ >= 1:
                        selB(i_ - 1)
            norm_evac(obr)
            V("dve", "tensor_tensor", ["obr", "gsig"], ["obr"], out=obr[:], in0=obr[:],
              in1=gsig[:].rearrange("p (h c) -> p h c", c=3)[:, :, 1:2].to_broadcast([128, 8, 64]), op=ALU.mult)
            V("dve", "tensor_tensor", ["obr", "unsa"], ["unsa"], out=unsa[:], in0=unsa[:], in1=obr[:], op=ALU.add)
            if STOP <= 3:
                return
            k0 = max(0, nk - 5)
            for kt in range(k0, nk):
                pp = kt % 2
                (kvn_, kvnn), (knb_, knbn), (knT_, knTn), (van_, vann) = B_KVN[pp], B_KNB[pp], B_KNT[pp], B_VAN[pp]
                load_nsa("win", kt, kvn_, kvnn)
                V("dve", "tensor_copy", [kvnn], [knbn], out=knb_[:], in_=kvn_[:, 0:128])
                transposes([knb_[:]], [knbn], knT_[:], knTn)
                ones_col(van_[:, :, 64:65], vann, kt, tv)
                V("pool", "tensor_copy", [kvnn], [vann], out=van_[:, :, 0:64], in_=kvn_[:, 128:256].rearrange("p (g d) -> p g d", g=2))
                for g in range(2):
                    (PT_, PTn), (pS_, pSn) = B_PT[g], B_PS[g]
                    S.op("pe", (lambda g, pS_, knT_: lambda e: e.matmul(out=pS_[:], lhsT=knT_[64 * g:64 * g + 64, :], rhs=qaT[64 * g:64 * g + 64, :],
                                                                        start=True, stop=True))(g, pS_, knT_), r=[knTn, "qaT"], w=[pSn])
                    V("act", "activation", [pSn], [PTn], out=PT_[:], in_=pS_[:], func=AF.Exp, scale=0.125)
                    if kt == kd:
                        sel_mask([], PTn, PT_[:].rearrange("p (r q) -> p r q", r=4), [[0, 4], [1, 128]], 0, -1)
                    if kt == nk - 5:
                        sel_mask([], PTn, PT_[:].rearrange("p (r q) -> p r q", r=4), [[0, 4], [-1, 128]], 0, 1)

                    def pv3(e, g=g, kt=kt, PT=PT_, vaugn=van_):
                        ins = None
                        for r in range(4):
                            ins = e.matmul(out=acc[:, g, r * 65:(r + 1) * 65], lhsT=PT[:, r * 128:(r + 1) * 128], rhs=vaugn[:, g, :],
                                           start=(kt == k0 and r == 0), stop=(kt == nk - 1 and r == 3))
                        return ins
                    S.op("pe", pv3, r=[PTn, vann], w=["acc"])
            norm_evac(obr)
            V("dve", "tensor_tensor", ["obr", "gsig"], ["obr"], out=obr[:], in0=obr[:],
              in1=gsig[:].rearrange("p (h c) -> p h c", c=3)[:, :, 2:3].to_broadcast([128, 8, 64]), op=ALU.mult)
            V("dve", "tensor_tensor", ["obr", "unsa"], ["unsa"], out=unsa[:], in0=unsa[:], in1=obr[:], op=ALU.add)
            V("dve", "tensor_tensor", ["unsa", "zs"], ["u"], out=u[:, 0:512], in0=unsa[:].rearrange("p h d -> p (h d)"), in1=zs[:, 0:512], op=ALU.mult)
            if STOP <= 4:
                return
            S.op("pe", lambda e: e.matmul(out=pC[:, 0:8], lhsT=sel127[:], rhs=cT[:, kd, :], start=True, stop=True), r=["cT", "sel127"], w=["pC"])
            V("dve", "tensor_copy", ["pC"], ["cref"], out=cref[:], in_=pC[:, 0:8])
            V("dve", "tensor_tensor", ["cref", "cT"], ["bias"], out=bias[:, 0:nk, :], in0=cref[:].unsqueeze(1).to_broadcast([128, nk, 8]),
              in1=cT[:, 0:nk, :], op=ALU.subtract)
            def foxA(kt, hp):
                pp = kt % 2
                (kvf_, kvfn), (kbb_, kbbn), (kbT_, kbTn), (vaf_, vafn) = B_KVF[pp], B_KBB[pp], B_KBT[pp], B_VAF[pp]
                if hp == 0:
                    load_fox(kt, kvf_, kvfn)
                    V("dve", "tensor_copy", [kvfn], [kbbn], out=kbb_[:], in_=kvf_[:, 0:512])
                    transposes([kbb_[:, r * 128:(r + 1) * 128] for r in range(4)], [kbbn], kbT_[:], kbTn)
                    ones_col(vaf_[:, :, 64:65], vafn, kt, tv)
                    V("pool", "tensor_copy", [kvfn], [vafn], out=vaf_[:, :, 0:64], in_=kvf_[:, 512:1024].rearrange("p (h d) -> p h d", h=8))
                (PT_, PTn), (pS_, pSn) = B_PT[hp], B_PS[hp]
                for hh in range(4):
                    h = 4 * hp + hh
                    pr, lo = h // 2, 64 * (h % 2)
                    S.op("pe", (lambda hh, pr, lo, pS_, kbT_: lambda e: e.matmul(out=pS_[:, hh * 128:(hh + 1) * 128],
                                                                                 lhsT=kbT_[lo:lo + 64, pr * 128:(pr + 1) * 128],
                                                                                 rhs=qbT[lo:lo + 64, pr * 128:(pr + 1) * 128],
                                                                                 start=True, stop=True))(hh, pr, lo, pS_, kbT_),
                         r=[kbTn, "qbT"], w=[pSn])
                for hh in range(4):
                    h = 4 * hp + hh
                    V("act", "activation", [pSn, "bias"], [PTn], out=PT_[:, hh * 128:(hh + 1) * 128], in_=pS_[:, hh * 128:(hh + 1) * 128],
                      func=AF.Exp, scale=0.125, bias=bias[:, kt, h:h + 1])

            def foxB(kt, hp):
                pp = kt % 2
                (vaf_, vafn) = B_VAF[pp]
                (PT_, PTn) = B_PT[hp]
                if kt == kd:
                    sel_mask([], PTn, PT_[:].rearrange("p (r q) -> p r q", r=4), [[0, 4], [1, 128]], 0, -1)

                def pv4(e, hp=hp, kt=kt, PT=PT_, vaugf=vaf_):
                    ins = None
                    for hh in range(4):
                        ins = e.matmul(out=acc[:, hp, hh * 65:(hh + 1) * 65], lhsT=PT[:, hh * 128:(hh + 1) * 128], rhs=vaugf[:, 4 * hp + hh, :],
                                       start=(kt == 0 and hh == 0), stop=(kt == nk - 1 and hh == 3))
                    return ins
                S.op("pe", pv4, r=[PTn, vafn], w=["acc"])
            units = [(kt, hp) for kt in range(nk) for hp in range(2)]
            for i_ in range(len(units) + 1):
                if i_ < len(units):
                    foxA(*units[i_])
                if i_ >= 1:
                    foxB(*units[i_ - 1])
            norm_evac(obr)
            V("dve", "tensor_tensor", ["obr", "zs"], ["u"], out=u[:, 512:1024], in0=obr[:].rearrange("p h d -> p (h d)"), in1=zs[:, 512:1024], op=ALU.mult)
            if STOP <= 5:
                return
            V("dve", "tensor_copy", ["u"], ["u_bf"], out=u_bf[:], in_=u[:])
            transposes([u_bf[:, c * 128:(c + 1) * 128] for c in range(8)], ["u_bf"], uT[:], "uT")
            xload()
            for hc in range(2):
                def omm(e, hc=hc):
                    ins = None
                    for c in range(8):
                        ins = e.matmul(out=pP[hc][:], lhsT=uT[:, c * 128:(c + 1) * 128], rhs=w_out_bf[:, c, hc * 512:(hc + 1) * 512],
                                       start=(c == 0), stop=(c == 7))
                    return ins
                S.op("pe", omm, r=["uT", "w_out_bf"], w=[f"pP{hc}"])
                V("dve", "tensor_tensor", [f"pP{hc}", "xt"], ["ysb"], out=ysb[:, hc * 512:(hc + 1) * 512], in0=pP[hc][:], in1=xt[:, hc * 512:(hc + 1) * 512], op=ALU.add)

        compress(255, 0, True)

        def p_load_nsa(which, kt, dst, dn):
            src = sc_slc if which == "slc" else sc_win
            dma("sp", dst[:], src[kt * 128:(kt + 1) * 128, :], [f"sc{which}{kt}"], [dn], dn)

        def p_load_fox(kt, dst, dn):
            dma("sp", dst[:], sc_fox[kt * 128:(kt + 1) * 128, :], [f"scfox{kt}"], [dn, "wstage"], dn)
        S.lastw["cT"] = S.lastw["c_all31"]
        S.readers["cT"] = []
        for j in range(npq):
            orow = slice(j * 128, (j + 1) * 128)

            def qload(orow=orow, j=j):
                dma("sp", proj[:, O_QA:NIN], qstore[orow, :], [f"qst{j}"], PRK + PRQ + ["projq"], "projq")

            def xload(j=j):
                xr = 2 * j + 1
                dma("sp", xt[:], x_in[xr * 128:(xr + 1) * 128, :], [], ["xt"], "xt")
            attn(2 * j + 2, (2 * j + 1) * 128, p_load_nsa, p_load_fox, c_all, 0, 0, qload, xload, True)
            dma("act", o_y[orow, :], ysb[:], ["ysb"], [f"oy{j}"], "st")

        for q_i in range(nseq):
            def s_load(dst, dname, pool_ap, sc_ap, kt, which, q_i=q_i):
                if kt < 16:
                    if which == "win":
                        dma("sp", dst, win_in[q_i, (kt - 12) * 128:(kt - 11) * 128, :], [], [dname], dname)
                    else:
                        ia = idx[:, q_i * 16 + kt:q_i * 16 + kt + 1]
                        S.op("pool", lambda e: e.indirect_dma_start(out=dst, out_offset=None, in_=pool_ap,
                                                                      in_offset=bass.IndirectOffsetOnAxis(ap=ia, axis=0)),
                             r=["idx"], w=[dname], dma="g_" + dname)
                else:
                    V("pool", "memset", [], [dname], ap=dst, constant=0.0)
                    dma("sp", dst[0:8, :], sc_ap[32 * 128 + 8 * q_i:32 * 128 + 8 * q_i + 8, :], [f"sc{which}32"], [dname], dname)

            for kt in range(16):
                s_load(kvn[:], "kvn", pool_cmp, sc_cmp, kt, "cmp")
                V("dve", "tensor_copy", ["kvn"], ["kv_bf"], out=kv_bf[:], in_=kvn[:])

                def tr3(e):
                    e.transpose(out=pT[:, 0:128], in_=kv_bf[:, 0:128], identity=ident[:])
                    return e.transpose(out=pT[:, 128:256], in_=kv_bf[:, 128:256], identity=ident[:])
                S.op("pe", tr3, r=["kv_bf", "ident"], w=["pT"])
                V("act", "copy", ["pT"], ["kcT"], out=kcT[:, kt * 128:(kt + 1) * 128], in_=pT[:, 0:128])
                V("act", "copy", ["pT"], ["vcT"], out=vcT[:, kt * 128:(kt + 1) * 128], in_=pT[:, 128:256])
            compress(127, 1, False)
            for kt in range(17):
                if kt < 16:
                    ia = idx[:, q_i * 16 + kt:q_i * 16 + kt + 1]
                    S.op("pool", (lambda kt, ia: lambda e: e.indirect_dma_start(out=lfs[:, kt, :], out_offset=None, in_=pool_lf,
                                                                                 in_offset=bass.IndirectOffsetOnAxis(ap=ia, axis=0)))(kt, ia),
                         r=["idx"], w=["lfs"], dma="g_lfs")
                else:
                    V("pool", "memset", [], ["lfs"], ap=lfs[:, 16, :], constant=0.0)
                    dma("sp", lfs[0:8, 16, :], sc_lf[8 * q_i:8 * q_i + 8, :], ["sc_lf"], ["lfs"], "lfs")
            V("pool", "memset", [], ["carry"], ap=carry[:], constant=0.0)
            for kt in range(17):
                def cmm2(e, kt=kt):
                    e.matmul(out=pC[:, 0:8], lhsT=triU[:], rhs=lfs[:, kt, :], start=True, stop=True)
                    return e.matmul(out=pC[:, 8:16], lhsT=ones_f[:], rhs=lfs[:, kt, :], start=True, stop=True)
                S.op("pe", cmm2, r=["lfs", "triU", "ones_f"], w=["pC"])
                V("dve", "tensor_tensor", ["pC", "carry"], ["cT"], out=c_seq[:, kt, :], in0=pC[:, 0:8], in1=carry[:], op=ALU.add)
                V("dve", "tensor_tensor", ["pC", "carry"], ["carry"], out=carry[:], in0=pC[:, 8:16], in1=carry[:], op=ALU.add)

            def s_load_nsa(which, kt, dst, dn, s_load=s_load):
                s_load(dst[:], dn, pool_slc if which == "slc" else None, sc_slc if which == "slc" else sc_win, kt, which)

            def s_load_fox(kt, dst, dn, s_load=s_load):
                s_load(dst[:], dn, pool_fox, sc_fox, kt, "fox")

            def qload_s(q_i=q_i):
                V("pool", "memset", [], PRK + PRQ + ["projq"], ap=proj[:, O_QA:NIN], constant=0.0)
                dma("sp", proj[0:8, O_QA:NIN], qstore[16 * 128 + 8 * q_i:16 * 128 + 8 * q_i + 8, :], ["qst16"], ["projq"], "projq")

            def xload_s(q_i=q_i):
                V("pool", "memset", [], ["xt"], ap=xt[:], constant=0.0)
                dma("sp", xt[0:8, :], x_in[32 * 128 + 8 * q_i:32 * 128 + 8 * q_i + 8, :], [], ["xt"], "xt")
            attn(17, 2048, s_load_nsa, s_load_fox, c_seq, 1, 1, qload_s, xload_s, False)
            dma("act", o_y[16 * 128 + 8 * q_i:16 * 128 + 8 * q_i + 8, :], ysb[0:8, :], ["ysb"], [f"oys{q_i}"], "st")

        print("[build] nsem", S.nsem, {e: len(q) for e, q in S.q.items()}, flush=True)
        with nc.Block() as block:
            S.emit(block)
    return nc


_NC = None
_DEBUG_RETURN_MAPS = False
STOP = 99


def _perm_cols():
    sizes = [512, 128, 128, 128, 128, 128, 128, 24, 512, 512, 512, 512, 8, 512]
    names = ["qa", "kc", "vc", "ks", "vs", "kw", "vw", "ga", "za", "qb", "kb", "vb", "fb", "zb"]
    offs = np.concatenate([[0], np.cumsum(sizes)])
    seg = {n: np.arange(offs[i], offs[i + 1]) for i, n in enumerate(names)}
    order = ["kc", "ks", "kw", "kb", "vc", "vs", "vw", "vb", "fb", "qa", "qb", "za", "zb", "ga"]
    return np.concatenate([seg[n] for n in order])


def kernel(x_prompt, x_sample, cache_nsa_cmp_kv, cache_nsa_slc_kv, cache_nsa_win_kv, cache_fox_kv, cache_fox_logf,
           page_table, g_norm, w_in, b_f, gq_a, gk_cmp, gk_slc, gk_win, pe_cmp_k, pe_cmp_v,
           w_cmp1_k, w_cmp2_k, w_cmp1_v, w_cmp2_v, gq_b, gk_b, w_out):
    global _NC
    f32 = np.float32
    A = lambda v: np.asarray(v, f32)
    x_prompt = A(x_prompt); x_sample = A(x_sample)
    perm = _perm_cols()
    w_p = np.ascontiguousarray(A(w_in)[0][:, perm])
    w_o = np.ascontiguousarray(A(w_out)[0])
    gn = np.ascontiguousarray(A(g_norm)[0].reshape(8, 128).T)
    rep = lambda v: np.ascontiguousarray(np.broadcast_to(A(v).reshape(1, -1), (128, A(v).size)))
    gk = rep(np.concatenate([np.tile(A(gk_slc)[0], 2), np.tile(A(gk_win)[0], 2), np.tile(A(gk_b)[0], 8)]))
    gq = rep(np.concatenate([np.tile(A(gq_a)[0], 8), np.tile(A(gq_b)[0], 8)]))
    gkc = rep(A(gk_cmp)[0])
    bfb = rep(A(b_f)[0])
    w1 = []
    for w_ in (w_cmp1_k, w_cmp1_v):
        a = A(w_)[0].transpose(1, 0, 2).reshape(64, 4096)
        w1.append(np.ascontiguousarray(np.concatenate([a, a], 0)))
    w2 = np.ascontiguousarray(np.concatenate([A(w_cmp2_k)[0], A(w_cmp2_v)[0]], 1))
    peT = np.ascontiguousarray(np.concatenate([A(pe_cmp_k)[0].T, A(pe_cmp_v)[0].T], 1))
    n = np.arange(256); jb = np.arange(64)
    ov = np.clip(np.minimum(n[:, None] * 16 + 32, jb[None, :] * 64 + 64) - np.maximum(n[:, None] * 16, jb[None, :] * 64), 0, None) / 32.0
    msel = np.ascontiguousarray(ov.astype(f32).reshape(2, 128, 64).transpose(1, 0, 2).reshape(128, 128))
    half = 32
    inv = np.power(np.float32(10000.0), -np.arange(half, dtype=f32) / half).astype(f32)
    win = A(cache_nsa_win_kv)[0].reshape(128, 512, 256)
    pool_cmp = A(cache_nsa_cmp_kv)[0].reshape(-1, 256)
    pool_slc = A(cache_nsa_slc_kv)[0].reshape(-1, 256)
    pool_fox = A(cache_fox_kv)[0].reshape(-1, 1024)
    pool_lf = A(cache_fox_logf)[0].reshape(-1, 8)
    pt = np.asarray(page_table, np.int32)
    piota = np.arange(128, dtype=f32).reshape(128, 1)
    in_maps = []
    for c in range(NCORES):
        b, h = c // 2, c % 2
        xb = x_prompt[b].reshape(32, 128, D)
        if h == 1:
            xst = xb
            nat = np.arange(32)
        else:
            xst = np.concatenate([np.zeros((1, 128, D), f32), xb[:31]], 0)
            nat = np.arange(32) - 1
        xsm = x_sample[16 * c:16 * c + 16].reshape(1, 128, D)
        xc = np.ascontiguousarray(np.concatenate([xst, xsm], 0).reshape(NS * 128, D))
        pos = np.zeros((NS, 128), f32)
        for s_ in range(32):
            pos[s_] = nat[s_] * 128 + np.arange(128)
        pos[32] = 2048 + (np.arange(128) % 8)
        ang = pos[:, :, None] * inv[None, None, :]
        cs_ = np.concatenate([np.cos(ang), np.sin(ang)], -1).astype(f32)
        cs_ = np.ascontiguousarray(cs_.transpose(1, 0, 2))
        cval = np.ones((128, 2), f32)
        f0 = np.full((128, 2, 64), -2.0, f32)
        dmin = np.full((128, 2, 64), 1e9, f32)
        tval = np.ones((128, 1), f32)
        if h == 0:
            cval[0:8, 0] = 0.0
            f0[:, 0, 2] = 1e4
            dmin[:, 0, 0:2] = -1.0
            tval[:] = 0.0
        else:
            f0[:, 0, 0] = 1e4
        f0[:, 1, 0] = 1e4
        ptc = np.ascontiguousarray(np.broadcast_to(pt[16 * c:16 * c + 16].reshape(1, 256), (128, 256))).astype(np.int32)
        in_maps.append({"x": xc, "w_in": w_p, "w_out": w_o, "gn": gn, "cs": cs_, "gk": gk, "gq": gq, "gkc": gkc, "bfb": bfb,
                        "win": np.ascontiguousarray(win[16 * c:16 * c + 16]), "w1k": w1[0], "w1v": w1[1], "w2": w2, "peT": peT,
                        "msel": msel, "cval": cval, "f0": np.ascontiguousarray(f0.reshape(128, 128)),
                        "dmin": np.ascontiguousarray(dmin.reshape(128, 128)), "pool_cmp": pool_cmp, "pool_slc": pool_slc,
                        "pool_fox": pool_fox, "pool_lf": pool_lf, "pt": ptc, "piota": piota, "tval": tval})
    if N_SAMPLE_SEQ == 0:
        for m in in_maps:
            for k in ("pool_cmp", "pool_slc", "pool_fox", "pool_lf"):
                m.pop(k)
    if _DEBUG_RETURN_MAPS:
        return in_maps
    if _NC is None:
        _NC = build()
    res = run_bass_kernel_spmd(_NC, in_maps, core_ids=list(range(NCORES)))
    R = res.results
    yp = np.zeros((4, 32, 128, D), f32); ys = np.zeros((128, 8, D), f32)
    cmp_p = np.zeros((4, 32, 128, 256), f32); slc_p = np.zeros((4, 32, 128, 256), f32); win_p = np.zeros((4, 32, 128, 256), f32)
    fox_p = np.zeros((4, 32, 128, 1024), f32); lf_p = np.zeros((4, 32, 128, 8), f32)
    cmp_s = np.zeros((128, 8, 256), f32); slc_s = np.zeros((128, 8, 256), f32); fox_s = np.zeros((128, 8, 1024), f32)
    lf_s = np.zeros((128, 8, 8), f32); win_s = np.zeros((128, 512, 256), f32)
    for c in range(NCORES):
        b, h = c // 2, c % 2
        r = R[c]
        for (dst_p, dst_s, key, wd) in ((yp, ys, "o_y", D), (cmp_p, cmp_s, "o_cmp", 256), (slc_p, slc_s, "o_slc", 256),
                                        (fox_p, fox_s, "o_fox", 1024), (lf_p, lf_s, "o_lf", 8)):
            a = np.asarray(r[key]).reshape(NQ, 128, wd)
            dst_p[b, h::2] = a[:16]
            dst_s[16 * c:16 * c + 16] = a[16].reshape(16, 8, wd)
        win_p[b, h::2] = np.asarray(r["o_win"]).reshape(NQ, 128, 256)[:16]
        win_s[16 * c:16 * c + 16] = np.asarray(r["o_wins"]).reshape(16, 512, 256)
    kv = lambda a, g: a.reshape(a.shape[:-1] + (2, g, 64))
    return (yp.reshape(4, T, D), ys,
            kv(cmp_p.reshape(1, 4, T, 256), 2), kv(cmp_s.reshape(1, 128, 8, 256), 2),
            kv(slc_p.reshape(1, 4, T, 256), 2), kv(slc_s.reshape(1, 128, 8, 256), 2),
            kv(win_p.reshape(1, 4, T, 256)[:, :, T - 512:], 2), kv(win_s.reshape(1, 128, 512, 256), 2),
            kv(fox_p.reshape(1, 4, T, 1024), 8), kv(fox_s.reshape(1, 128, 8, 1024), 8),
            lf_p.reshape(1, 4, T, 8), lf_s.reshape(1, 128, 8, 8))
```
